# Optimizing a Trainium2 kernel written in Bass

```python
import math
import jax, jax.numpy as jnp
from jax import lax
import numpy as np

D_MODEL = 2048
BATCH = 16
SEQ = 2048
DEPTH = 1
DEC_BATCH = 8
DEC_SEQ = 4096
PAST_LEN = 128

DA_HEADS = 8
DA_DK = 64
DA_DV = 2 * DA_DK
SW_Q_HEADS = 16
SW_KV_HEADS = 4
SW_GROUP = SW_Q_HEADS // SW_KV_HEADS
SW_DH = 64
WINDOW = 128
BLOCK = 128
N_BUCKETS = 32
MAX_DISTANCE = 128
N_BIAS_HEADS = DA_HEADS + SW_Q_HEADS
D_FF = 5632
EPS = 1e-6

DA_Q = DA_HEADS * 2 * DA_DK
DA_K = DA_HEADS * 2 * DA_DK
DA_V = DA_HEADS * DA_DV
SW_Q = SW_Q_HEADS * SW_DH
SW_K = SW_KV_HEADS * SW_DH
SW_V = SW_KV_HEADS * SW_DH
D_IN = DA_Q + DA_K + DA_V + SW_Q + SW_K + SW_V
SPLITS = list(np.cumsum([DA_Q, DA_K, DA_V, SW_Q, SW_K]))
D_MIX = DA_HEADS * DA_DV + SW_Q_HEADS * SW_DH

kernel_name = "hybrid_diffattn_swa_macaron_encoder"


def rmsnorm(x, g):
    xf = x.astype(jnp.float32)
    y = xf * lax.rsqrt(jnp.mean(xf * xf, axis=-1, keepdims=True) + EPS)
    return (y * g.astype(jnp.float32)).astype(x.dtype)


def swiglu(x, w_gu, w_down):
    gu = x @ w_gu
    g, u = jnp.split(gu, 2, axis=-1)
    return (jax.nn.silu(g) * u) @ w_down


def t5_bucket(rp):
    half = N_BUCKETS // 2
    max_exact = half // 2
    ret = jnp.where(rp > 0, half, 0)
    n = jnp.abs(rp)
    nf = jnp.maximum(n, 1).astype(jnp.float32)
    large = max_exact + (jnp.log(nf / max_exact) / math.log(MAX_DISTANCE / max_exact)
                         * (half - max_exact)).astype(jnp.int32)
    large = jnp.minimum(large, half - 1)
    return ret + jnp.where(n < max_exact, n, large)


def diff_attention(q, k, v, lam, subln_g, lambda_init, rel_bias):
    B, S = q.shape[0], q.shape[1]
    nb = S // BLOCK
    scale = DA_DK ** -0.5
    qb = q.reshape(B, nb, BLOCK, DA_HEADS, 2, DA_DK).transpose(1, 0, 3, 4, 2, 5)
    kpos = jnp.arange(S)
    bias_tab = rel_bias[:, :DA_HEADS]

    def block(args):
        qi, i = args
        qpos = i * BLOCK + jnp.arange(BLOCK)
        bias = bias_tab[t5_bucket(kpos[None, :] - qpos[:, None])].astype(jnp.float32)
        s = jnp.einsum('bhmqd,bkhmd->bhmqk', qi, k).astype(jnp.float32) * scale \
            + bias.transpose(2, 0, 1)[None, :, None]
        p = jax.nn.softmax(s, axis=-1)
        a = p[:, :, 0] - lam * p[:, :, 1]
        return jnp.einsum('bhqk,bkhd->bqhd', a.astype(v.dtype), v)

    o = lax.map(block, (qb, jnp.arange(nb)))
    o = o.transpose(1, 0, 2, 3, 4).reshape(B, S, DA_HEADS, DA_DV)
    o = rmsnorm(o, subln_g) * (1.0 - lambda_init)
    return o.reshape(B, S, DA_HEADS * DA_DV)


def window_attention(q, k, v, sink, rel_bias):
    B, S = q.shape[0], q.shape[1]
    nb = S // BLOCK
    scale = SW_DH ** -0.5
    qb = jnp.moveaxis(q.reshape(B, nb, BLOCK, SW_KV_HEADS, SW_GROUP, SW_DH), 1, 0)

    def band(t):
        tp = jnp.pad(t, ((0, 0), (BLOCK, BLOCK), (0, 0), (0, 0)))
        tp = tp.reshape(B, nb + 2, BLOCK, SW_KV_HEADS, SW_DH)
        tb = jnp.concatenate([tp[:, :-2], tp[:, 1:-1], tp[:, 2:]], axis=2)
        return jnp.moveaxis(tb, 1, 0)

    kb, vb = band(k), band(v)
    off = jnp.arange(3 * BLOCK) - BLOCK
    rp = off[None, :] - jnp.arange(BLOCK)[:, None]
    kpos = jnp.arange(nb)[:, None] * BLOCK + off[None, :]
    valid = (jnp.abs(rp) <= WINDOW)[None] & ((kpos >= 0) & (kpos < S))[:, None, :]
    bias = rel_bias[t5_bucket(rp)][..., DA_HEADS:].astype(jnp.float32)
    bias = bias.transpose(2, 0, 1).reshape(SW_KV_HEADS, SW_GROUP, BLOCK, 3 * BLOCK)
    sk = sink.astype(jnp.float32).reshape(SW_KV_HEADS, SW_GROUP)[:, :, None, None]

    def block(args):
        qi, ki, vi, ok = args
        s = jnp.einsum('bqhgd,bkhd->bhgqk', qi, ki).astype(jnp.float32) * scale + bias
        s = jnp.where(ok[None, None, None], s, -jnp.inf)
        m = jnp.maximum(jnp.max(s, axis=-1, keepdims=True), sk)
        e = jnp.exp(s - m)
        p = e / (jnp.sum(e, axis=-1, keepdims=True) + jnp.exp(sk - m))
        return jnp.einsum('bhgqk,bkhd->bqhgd', p.astype(vi.dtype), vi)

    o = lax.map(block, (qb, kb, vb, valid))
    return jnp.moveaxis(o, 0, 1).reshape(B, S, SW_Q_HEADS * SW_DH)


def encoder_layer(x, l, rel_bias, g_ffn1_pre, w_ffn1_gu, w_ffn1_down, g_ffn1_post,
                  g_mix_pre, w_in, lambda_q1, lambda_k1, lambda_q2, lambda_k2,
                  g_diff_subln, sink_logit, w_out, g_mix_post,
                  g_ffn2_pre, w_ffn2_gu, w_ffn2_down, g_ffn2_post):
    B, S = x.shape[0], x.shape[1]
    lambda_init = 0.8 - 0.6 * math.exp(-0.3 * l)
    h = x + 0.5 * rmsnorm(swiglu(rmsnorm(x, g_ffn1_pre), w_ffn1_gu, w_ffn1_down), g_ffn1_post)
    n = rmsnorm(h, g_mix_pre)
    proj = n @ w_in
    q_da, k_da, v_da, q_sw, k_sw, v_sw = jnp.split(proj, SPLITS, axis=-1)
    lam = (jnp.exp(jnp.sum(lambda_q1.astype(jnp.float32) * lambda_k1.astype(jnp.float32)))
           - jnp.exp(jnp.sum(lambda_q2.astype(jnp.float32) * lambda_k2.astype(jnp.float32)))
           + lambda_init)
    o_da = diff_attention(q_da.reshape(B, S, DA_HEADS, 2, DA_DK),
                          k_da.reshape(B, S, DA_HEADS, 2, DA_DK),
                          v_da.reshape(B, S, DA_HEADS, DA_DV),
                          lam, g_diff_subln, lambda_init, rel_bias)
    o_sw = window_attention(q_sw.reshape(B, S, SW_Q_HEADS, SW_DH),
                            k_sw.reshape(B, S, SW_KV_HEADS, SW_DH),
                            v_sw.reshape(B, S, SW_KV_HEADS, SW_DH),
                            sink_logit, rel_bias)
    mix = jnp.concatenate([o_da, o_sw], axis=-1) @ w_out
    h = h + rmsnorm(mix, g_mix_post)
    return h + 0.5 * rmsnorm(swiglu(rmsnorm(h, g_ffn2_pre), w_ffn2_gu, w_ffn2_down), g_ffn2_post)


def setup_inputs(seed: int = 0) -> dict:
    key = jax.random.key(seed)
    ks = jax.random.split(key, 24)
    f32 = jnp.float32

    def nrm(k, shape, s):
        return jax.random.normal(k, shape, f32) * s

    def gain(k, shape):
        return 1.0 + 0.02 * jax.random.normal(k, shape, f32)

    return {
        "x_prompt": nrm(ks[0], (BATCH, SEQ, D_MODEL), 1.0),
        "x_sample": nrm(ks[1], (DEC_BATCH, DEC_SEQ, D_MODEL), 1.0),
        "rel_bias": nrm(ks[2], (N_BUCKETS, N_BIAS_HEADS), 0.2),
        "g_ffn1_pre": gain(ks[3], (DEPTH, D_MODEL)),
        "w_ffn1_gu": nrm(ks[4], (DEPTH, D_MODEL, 2 * D_FF), D_MODEL ** -0.5),
        "w_ffn1_down": nrm(ks[5], (DEPTH, D_FF, D_MODEL), D_FF ** -0.5),
        "g_ffn1_post": gain(ks[6], (DEPTH, D_MODEL)),
        "g_mix_pre": gain(ks[7], (DEPTH, D_MODEL)),
        "w_in": nrm(ks[8], (DEPTH, D_MODEL, D_IN), D_MODEL ** -0.5),
        "lambda_q1": nrm(ks[9], (DEPTH, DA_DK), 0.1),
        "lambda_k1": nrm(ks[10], (DEPTH, DA_DK), 0.1),
        "lambda_q2": nrm(ks[11], (DEPTH, DA_DK), 0.1),
        "lambda_k2": nrm(ks[12], (DEPTH, DA_DK), 0.1),
        "g_diff_subln": gain(ks[13], (DEPTH, DA_DV)),
        "sink_logit": nrm(ks[14], (DEPTH, SW_Q_HEADS), 0.5),
        "w_out": nrm(ks[15], (DEPTH, D_MIX, D_MODEL), D_MIX ** -0.5),
        "g_mix_post": gain(ks[16], (DEPTH, D_MODEL)),
        "g_ffn2_pre": gain(ks[17], (DEPTH, D_MODEL)),
        "w_ffn2_gu": nrm(ks[18], (DEPTH, D_MODEL, 2 * D_FF), D_MODEL ** -0.5),
        "w_ffn2_down": nrm(ks[19], (DEPTH, D_FF, D_MODEL), D_FF ** -0.5),
        "g_ffn2_post": gain(ks[20], (DEPTH, D_MODEL)),
    }


def reference(x_prompt, x_sample, rel_bias, g_ffn1_pre, w_ffn1_gu, w_ffn1_down, g_ffn1_post,
              g_mix_pre, w_in, lambda_q1, lambda_k1, lambda_q2, lambda_k2,
              g_diff_subln, sink_logit, w_out, g_mix_post,
              g_ffn2_pre, w_ffn2_gu, w_ffn2_down, g_ffn2_post):
    def trunk(x):
        for l in range(DEPTH):
            x = encoder_layer(x, l, rel_bias, g_ffn1_pre[l], w_ffn1_gu[l], w_ffn1_down[l],
                              g_ffn1_post[l], g_mix_pre[l], w_in[l], lambda_q1[l], lambda_k1[l],
                              lambda_q2[l], lambda_k2[l], g_diff_subln[l], sink_logit[l],
                              w_out[l], g_mix_post[l], g_ffn2_pre[l], w_ffn2_gu[l],
                              w_ffn2_down[l], g_ffn2_post[l])
        return x
    y_prompt = trunk(x_prompt)
    y_sample = trunk(x_sample)
    return (y_prompt, y_sample)
```

```python
import math
import contextlib
import numpy as np
import ml_dtypes
import concourse.bass as bass
import concourse.mybir as mybir
from concourse.bass_utils import run_bass_kernel_spmd

F32 = mybir.dt.float32
BF16 = mybir.dt.bfloat16
AF = mybir.ActivationFunctionType
ALU = mybir.AluOpType
AX = mybir.AxisListType

D = 2048
DFF = 5632
DIN = 4608
NKC = 16
NFC = 44
EPS = 1e-6
LAMBDA_INIT = 0.8 - 0.6 * math.exp(-0.3 * 0)
TT = 512
NSLOT = 5
MASKV = -30000.0

U_GU1, U_D1, U_INF, U_INT, U_OUT, U_GU2, U_D2 = 0, 44, 68, 81, 86, 94, 138
NUNIT = 162
LDA = 1280
LSW = 512


class Res:
    __slots__ = ("w", "rs")

    def __init__(self):
        self.w = None
        self.rs = []


class Op:
    __slots__ = ("eng", "emit", "deps", "flag", "cnt", "lane", "lval")

    def __init__(self, eng, emit):
        self.eng = eng
        self.emit = emit
        self.deps = []
        self.flag = False
        self.cnt = None
        self.lane = None
        self.lval = None


class Prog:
    ENGS = ("pe", "act", "dve", "pool", "sp")

    def __init__(self):
        self.ops = {e: [] for e in self.ENGS}
        self.lanes = {}
        self.last_lane_op = {}

    def _add(self, op, reads, writes):
        deps = []
        for r in reads:
            if r.w is not None:
                deps.append(r.w)
        for w in writes:
            if w.w is not None:
                deps.append(w.w)
            deps.extend(w.rs)
        seen = set()
        for d in deps:
            if d is op or id(d) in seen:
                continue
            seen.add(id(d))
            if d.lane is None and d.eng == "pe" and op.eng == "pe" and op.lane is None:
                continue
            op.deps.append(d)
            if d.lane is None:
                d.flag = True
        for w in writes:
            w.w = op
            w.rs = []
        for r in reads:
            r.rs.append(op)
        self.ops[op.eng].append(op)
        return op

    def op(self, eng, emit, reads=(), writes=()):
        return self._add(Op(eng, emit), reads, writes)

    def dma(self, eng, lane, emit, ndma=1, reads=(), writes=()):
        o = Op(eng, emit)
        o.lane = lane
        self.lanes[lane] = self.lanes.get(lane, 0) + 16 * ndma
        o.lval = self.lanes[lane]
        self.last_lane_op[lane] = o
        return self._add(o, reads, writes)

    def barrier(self):
        lasts = []
        for e in self.ENGS:
            for o in reversed(self.ops[e]):
                if o.lane is None and o.emit is not None:
                    lasts.append(o)
                    break
        lane_ops = list(self.last_lane_op.values())
        for e in self.ENGS:
            o = Op(e, None)
            for d in lasts:
                if d.eng != e:
                    o.deps.append(d)
                    d.flag = True
            o.deps.extend(lane_ops)
            self.ops[e].append(o)

    def emit_all(self, nc):
        for e in self.ENGS:
            c = 0
            for o in self.ops[e]:
                if o.lane is None and o.flag:
                    c += 1
                    o.cnt = c
        with contextlib.ExitStack() as st:
            esem = {e: st.enter_context(nc.semaphore("s_" + e)) for e in self.ENGS}
            lsem = {l: st.enter_context(nc.semaphore("l_" + l)) for l in self.lanes}
            block = st.enter_context(nc.Block())

            def run(e, engobj):
                waited = {}
                for o in self.ops[e]:
                    for d in o.deps:
                        if d.lane is None:
                            key = ("e", d.eng)
                            sem = esem[d.eng]
                            val = d.cnt
                        else:
                            key = ("l", d.lane)
                            sem = lsem[d.lane]
                            val = d.lval
                        if waited.get(key, 0) >= val:
                            continue
                        waited[key] = val
                        engobj.wait_ge(sem, val)
                    if o.emit is None:
                        continue
                    if o.lane is None:
                        ins = o.emit(engobj)
                        if o.flag:
                            ins.then_inc(esem[e], 1)
                    else:
                        o.emit(engobj, lsem[o.lane])
                if e == "sp":
                    for l, v in self.lanes.items():
                        if waited.get(("l", l), 0) < v:
                            engobj.wait_ge(lsem[l], v)

            @block.tensor
            def _(t):
                run("pe", t)

            @block.scalar
            def _(s):
                run("act", s)

            @block.vector
            def _(v):
                run("dve", v)

            @block.gpsimd
            def _(g):
                run("pool", g)

            @block.sync
            def _(s):
                run("sp", s)


class Buf:
    __slots__ = ("ap", "res", "lane")

    def __init__(self, ap, lane=None):
        self.ap = ap
        self.res = Res()
        self.lane = lane


def _t5_bucket_np(rp):
    rp = np.asarray(rp, np.int32)
    half = 16
    max_exact = 8
    ret = np.where(rp > 0, half, 0).astype(np.int32)
    n = np.abs(rp)
    nf = np.maximum(n, 1).astype(np.float32)
    large = max_exact + (np.log(nf / np.float32(max_exact)) / np.float32(math.log(128 / max_exact))
                         * np.float32(half - max_exact)).astype(np.int32)
    large = np.minimum(large, half - 1)
    return ret + np.where(n < max_exact, n, large)


def _onehot_tables():
    t_da = np.arange(LDA) - 639
    oh_da = np.zeros((33, LDA), np.float32)
    oh_da[_t5_bucket_np(t_da), np.arange(LDA)] = 1.0
    t_sw = np.arange(LSW) - 255
    oh_sw = np.zeros((33, LSW), np.float32)
    oh_sw[_t5_bucket_np(t_sw), np.arange(LSW)] = 1.0
    oh_sw[32, :] = np.where(np.abs(t_sw) > 128, MASKV, 0.0)
    return oh_da.astype(ml_dtypes.bfloat16), oh_sw.astype(ml_dtypes.bfloat16)


def build(seqs, debug=False, phases="0ABC", stop=None, ntiles=None):
    nc = bass.Bass("TRN2", target_bir_lowering=False)
    TOK = sum(seqs)
    NT = TOK // TT
    NBLK = TOK // 128
    seq_off = [sum(seqs[:i]) for i in range(len(seqs))]
    P = Prog()
    skind = "ExternalOutput" if debug else "Internal"

    def din(name, shape, dt=F32):
        return nc.dram_tensor(name, shape, dt, kind="ExternalInput")

    x_d = din("x", [TOK, D])
    y_d = nc.dram_tensor("y", [TOK, D], F32, kind="ExternalOutput")
    wgu_d = [din("w_ffn1_gu", [D, 2 * DFF]), din("w_ffn2_gu", [D, 2 * DFF])]
    wdn_d = [din("w_ffn1_down", [DFF, D]), din("w_ffn2_down", [DFF, D])]
    win_d = din("w_in", [D, DIN])
    wout_d = din("w_out", [D, D])
    relb_d = din("rel_bias", [32, 24])
    g_pre_d = [din("g_ffn1_pre", [1, D]), din("g_mix_pre", [1, D]), din("g_ffn2_pre", [1, D])]
    g_post_d = [din("g_ffn1_post", [1, D]), din("g_mix_post", [1, D]), din("g_ffn2_post", [1, D])]
    lam_d = [din("lambda_q1", [1, 64]), din("lambda_k1", [1, 64]), din("lambda_q2", [1, 64]), din("lambda_k2", [1, 64])]
    gsub_d = din("g_diff_subln", [1, 128])
    sink_d = din("sink_logit", [1, 16])
    ohda_d = din("oh_da", [33, LDA], BF16)
    ohsw_d = din("oh_sw", [33, LSW], BF16)

    wscr = nc.dram_tensor("wscr", [NUNIT, 128, 4096], BF16, kind=skind)
    hscr = nc.dram_tensor("hscr", [TOK, D], F32, kind=skind)
    qda = nc.dram_tensor("qda", [8, 128, TOK], BF16, kind=skind)
    kda = nc.dram_tensor("kda", [8, 128, TOK], BF16, kind=skind)
    vda = nc.dram_tensor("vda", [8, TOK, 128], BF16, kind=skind)
    qsw = nc.dram_tensor("qsw", [2, 128, NBLK, 4, 128], BF16, kind=skind)
    ksw = nc.dram_tensor("ksw", [2, 128, TOK], BF16, kind=skind)
    vsw = nc.dram_tensor("vsw", [4, TOK, 64], BF16, kind=skind)
    mixT = nc.dram_tensor("mixT", [16, 128, TOK], BF16, kind=skind)
    tabscr = nc.dram_tensor("tabscr", [24, 2048], BF16, kind="Internal")

    def DAP(t, off, pat):
        return bass.AP(t, off, [list(p) for p in pat])

    with contextlib.ExitStack() as st:
        NBYTES = 200 * 1024
        big = st.enter_context(nc.sbuf_tensor("big", [128, NBYTES // 2], BF16))
        psb = st.enter_context(nc.psum_tensor("psb", [128, 8, 512], F32))

        class Arena:
            def __init__(self):
                self.off = 0

            def alloc(self, nbytes, align=64):
                self.off = (self.off + align - 1) // align * align
                o = self.off
                self.off += nbytes
                assert self.off <= NBYTES, ("SBUF overflow", self.off)
                return o

        A = Arena()

        def bfv(nelem):
            o = A.alloc(nelem * 2)
            return big[:, o // 2:o // 2 + nelem]

        def f32v(nelem):
            o = A.alloc(nelem * 4)
            return big[:, o // 2:o // 2 + 2 * nelem].bitcast(F32)

        def bank(b):
            return psb[:, b, :]

        def bank_bf(b):
            return psb[:, b, :].bitcast(BF16)

        bank_res = [Res() for _ in range(8)]

        ident = Buf(bfv(128))
        identf = f32v(128)
        P.op("pool", lambda e: e.memset(identf, 0.0), writes=[ident.res])
        P.op("pool", lambda e: e.affine_select(out=identf, in_=identf, pattern=[[-1, 128]], compare_op=ALU.not_equal,
                                               fill=1.0, base=0, channel_multiplier=1), reads=[ident.res], writes=[ident.res])
        P.op("dve", lambda e: e.tensor_copy(out=ident.ap, in_=identf), reads=[ident.res], writes=[ident.res])

        stats = f32v(256)
        stats_res = [Res() for _ in range(16)]
        gpostA = Buf(f32v(D), "gpa")
        gpostB = Buf(f32v(D), "gpb")
        small = Buf(f32v(512), "small")
        gsub = Buf(f32v(128), "gsub")
        lamt = f32v(4 * 64)
        lam_res = Res()
        for i in range(4):
            P.dma("pool", "small", lambda e, s, i=i: e.dma_start(out=lamt[:, i * 64:(i + 1) * 64], in_=DAP(lam_d[i], 0, [[0, 128], [1, 64]])).then_inc(s, 16),
                  writes=[lam_res])
        P.dma("pool", "small", lambda e, s: e.dma_start(out=small.ap[:, 8:24], in_=DAP(sink_d, 0, [[0, 128], [1, 16]])).then_inc(s, 16), writes=[small.res])
        P.dma("pool", "small", lambda e, s: e.dma_start(out=small.ap[:, 32:40], in_=DAP(relb_d, 15 * 24, [[0, 128], [1, 8]])).then_inc(s, 16), writes=[small.res])
        P.dma("pool", "small", lambda e, s: e.dma_start(out=small.ap[:, 40:48], in_=DAP(relb_d, 31 * 24, [[0, 128], [1, 8]])).then_inc(s, 16), writes=[small.res])
        P.dma("pool", "gsub", lambda e, s: e.dma_start(out=gsub.ap, in_=DAP(gsub_d, 0, [[0, 128], [1, 128]])).then_inc(s, 16), writes=[gsub.res])
        P.op("dve", lambda e: e.tensor_tensor(out=lamt[:, 0:64], in0=lamt[:, 0:64], in1=lamt[:, 64:128], op=ALU.mult), reads=[lam_res], writes=[lam_res])
        P.op("dve", lambda e: e.tensor_tensor(out=lamt[:, 128:192], in0=lamt[:, 128:192], in1=lamt[:, 192:256], op=ALU.mult), reads=[lam_res], writes=[lam_res])
        P.op("dve", lambda e: e.reduce_sum(out=small.ap[:, 1:2], in_=lamt[:, 0:64], axis=AX.X), reads=[lam_res, small.res], writes=[small.res])
        P.op("dve", lambda e: e.reduce_sum(out=small.ap[:, 2:3], in_=lamt[:, 128:192], axis=AX.X), reads=[lam_res, small.res], writes=[small.res])
        P.op("act", lambda e: e.activation(out=small.ap[:, 1:3], in_=small.ap[:, 1:3], func=AF.Exp), reads=[small.res], writes=[small.res])
        P.op("act", lambda e: e.activation(out=small.ap[:, 8:24], in_=small.ap[:, 8:24], func=AF.Exp), reads=[small.res], writes=[small.res])
        P.op("dve", lambda e: e.scalar_tensor_tensor(out=small.ap[:, 0:1], in0=small.ap[:, 2:3], scalar=-LAMBDA_INIT, in1=small.ap[:, 1:2],
                                                     op0=ALU.add, op1=ALU.subtract), reads=[small.res], writes=[small.res])
        P.op("dve", lambda e: e.tensor_scalar(out=gsub.ap, in0=gsub.ap, scalar1=1.0 - LAMBDA_INIT, scalar2=None, op0=ALU.mult),
             reads=[gsub.res], writes=[gsub.res])
        neglam = small.ap[:, 0:1]

        persist_mark = A.off

        unit_res = [Res() for _ in range(NUNIT)]

        def wunit_ap(u):
            return DAP(wscr, u * 128 * 4096, [[4096, 128], [1, 4096]])

        if "0" in phases:
            A.off = persist_mark
            gpre = Buf(f32v(48), "gpre")
            for i in range(3):
                P.dma("pool", "gpre", lambda e, s, i=i: e.dma_start(out=gpre.ap[:, i * 16:(i + 1) * 16],
                                                                    in_=DAP(g_pre_d[i], 0, [[1, 128], [128, 16]]), allow_slow_non_contiguous=True).then_inc(s, 16), writes=[gpre.res])
            NST = 3
            stg = [Buf(f32v(4096).rearrange("p (a b) -> p a b", a=16), "st%d" % i) for i in range(NST)]
            NOB = 4
            obs = [Buf(bfv(4096).rearrange("p (a b) -> p a b", a=16), "ob%d" % i) for i in range(NOB)]
            cnt = {"st": 0, "ob": 0, "eng": 0}

            def next_st():
                b = stg[cnt["st"] % NST]
                cnt["st"] += 1
                return b

            def next_ob():
                b = obs[cnt["ob"] % NOB]
                cnt["ob"] += 1
                return b

            def cast(out_ap, in_ap, gi, reads, writes, nk=16):
                engs = ("dve", "pool")
                eng = engs[cnt["eng"] % 2]
                cnt["eng"] += 1
                if gi is None:
                    P.op(eng, lambda e: e.tensor_copy(out=out_ap, in_=in_ap), reads=reads, writes=writes)
                else:
                    gv = gpre.ap[:, gi * 16:gi * 16 + nk]
                    gb = bass.AP(gv.tensor, gv.offset, [list(gv.ap[0]), [1, nk], [0, in_ap.shape[2]]])
                    P.op(eng, lambda e: e.tensor_tensor(out=out_ap, in0=in_ap, in1=gb, op=ALU.mult), reads=list(reads) + [gpre.res], writes=writes)

            def store_unit(ob, u):
                P.dma("pool", ob.lane, lambda e, s: e.dma_start(out=wunit_ap(u), in_=ob.ap.rearrange("p a b -> p (a b)")).then_inc(s, 16),
                      reads=[ob.res], writes=[unit_res[u]])

            def conv_gu(w_d, ubase, gi):
                for jp in range(22):
                    sg_, su_ = next_st(), next_st()
                    P.dma("sp", sg_.lane, lambda e, s, sg_=sg_, jp=jp: e.dma_start(
                        out=sg_.ap, in_=DAP(w_d, jp * 256, [[2 * DFF, 128], [128 * 2 * DFF, 16], [1, 256]])).then_inc(s, 16), writes=[sg_.res])
                    P.dma("sp", su_.lane, lambda e, s, su_=su_, jp=jp: e.dma_start(
                        out=su_.ap, in_=DAP(w_d, DFF + jp * 256, [[2 * DFF, 128], [128 * 2 * DFF, 16], [1, 256]])).then_inc(s, 16), writes=[su_.res])
                    for q in range(2):
                        ob = next_ob()
                        cast(ob.ap[:, :, 0:128], sg_.ap[:, :, q * 128:(q + 1) * 128], gi, [sg_.res], [ob.res])
                        cast(ob.ap[:, :, 128:256], su_.ap[:, :, q * 128:(q + 1) * 128], gi, [su_.res], [ob.res])
                        store_unit(ob, ubase + 2 * jp + q)

            def conv_down(w_d, ubase):
                for dc in range(8):
                    for fu in range(3):
                        nf = 16 if fu < 2 else 12
                        s_ = next_st()
                        P.dma("sp", s_.lane, lambda e, s, s_=s_, dc=dc, fu=fu, nf=nf: e.dma_start(
                            out=s_.ap[:, 0:nf, :], in_=DAP(w_d, fu * 16 * 128 * D + dc * 256, [[D, 128], [128 * D, nf], [1, 256]])).then_inc(s, 16),
                            writes=[s_.res])
                        ob = next_ob()
                        cast(ob.ap[:, 0:nf, :], s_.ap[:, 0:nf, :], None, [s_.res], [ob.res])
                        store_unit(ob, ubase + dc * 3 + fu)

            def conv_in():
                chunks = []
                for h in range(8):
                    chunks.append([(0, h * 128, 128)])
                for h in range(8):
                    chunks.append([(0, 1024 + h * 128, 128)])
                for j in range(8):
                    pi, g = j // 4, j % 4
                    chunks.append([(0, 3072 + ((2 * pi) * 4 + g) * 64, 64), (64, 3072 + ((2 * pi + 1) * 4 + g) * 64, 64)])
                for c in range(2):
                    chunks.append([(0, 4096 + c * 128, 128)])
                for u in range(13):
                    s_ = next_st()
                    pieces = []
                    for c2 in range(2):
                        for (d0, s0, n) in chunks[2 * u + c2]:
                            pieces.append((c2 * 128 + d0, s0, n))
                    for (d0, s0, n) in pieces:
                        P.dma("sp", s_.lane, lambda e, s, s_=s_, d0=d0, s0=s0, n=n: e.dma_start(
                            out=s_.ap[:, :, d0:d0 + n], in_=DAP(win_d, s0, [[DIN, 128], [128 * DIN, 16], [1, n]])).then_inc(s, 16), writes=[s_.res])
                    ob = next_ob()
                    cast(ob.ap, s_.ap, 1, [s_.res], [ob.res])
                    store_unit(ob, U_INF + u)
                for u in range(5):
                    s0 = 2048 + u * 256 if u < 4 else 4352
                    s_ = next_st()
                    P.dma("sp", s_.lane, lambda e, s, s_=s_, s0=s0: e.dma_start(
                        out=s_.ap, in_=DAP(win_d, s0, [[DIN, 128], [128 * DIN, 16], [1, 256]])).then_inc(s, 16), writes=[s_.res])
                    ob = next_ob()
                    cast(ob.ap, s_.ap, 1, [s_.res], [ob.res])
                    store_unit(ob, U_INT + u)

            def conv_out():
                for dc in range(8):
                    s_ = next_st()
                    P.dma("sp", s_.lane, lambda e, s, s_=s_, dc=dc: e.dma_start(
                        out=s_.ap[:, 0:8, :], in_=DAP(wout_d, dc * 256, [[D, 128], [128 * D, 8], [1, 256]])).then_inc(s, 16), writes=[s_.res])
                    for pi in range(2):
                        for half in range(2):
                            r0 = 1024 + (2 * pi + half) * 256
                            P.dma("sp", s_.lane, lambda e, s, s_=s_, dc=dc, pi=pi, half=half, r0=r0: e.dma_start(
                                out=s_.ap[half * 64:(half + 1) * 64, 8 + pi * 4:12 + pi * 4, :],
                                in_=DAP(wout_d, r0 * D + dc * 256, [[D, 64], [64 * D, 4], [1, 256]])).then_inc(s, 16), writes=[s_.res])
                    ob = next_ob()
                    cast(ob.ap, s_.ap, None, [s_.res], [ob.res])
                    store_unit(ob, U_OUT + dc)

            conv_gu(wgu_d[0], U_GU1, 0)
            conv_down(wdn_d[0], U_D1)
            conv_in()
            conv_out()
            conv_gu(wgu_d[1], U_GU2, 2)
            conv_down(wdn_d[1], U_D2)
            P.barrier()

        def setup_ffn_region():
            A.off = persist_mark
            R = {}
            R["xres"] = [Buf(f32v(D), "xres%d" % i) for i in range(2)]
            R["ffo"] = [Buf(f32v(D), None) for i in range(4)]
            R["xnb"] = [Buf(bfv(D), None) for i in range(1)]
            R["xnT"] = Buf(bfv(16 * 512).rearrange("p (a b) -> p a b", a=16), "xnT")
            R["actT"] = Buf(bfv(NFC * 512).rearrange("p (a b) -> p a b", a=NFC), None)
            R["wslot"] = [Buf(bfv(4096).rearrange("p (a b) -> p a b", a=16), "w%d" % i) for i in range(NSLOT)]
            R["sg"] = [Buf(f32v(512), None) for i in range(2)]
            R["stage"] = [Buf(bfv(512), "stg%d" % i) for i in range(4)]
            R["junk"] = Buf(bfv(D), None)
            R["ssd"] = Buf(f32v(64), None)
            return R

        class WStream:
            def __init__(self, R, seq):
                self.slots = R["wslot"]
                self.seq = seq
                self.nload = 0
                self.nuse = 0

            def _load(self):
                n = self.nload
                u = self.seq[n]
                sl = self.slots[n % NSLOT]
                P.dma("sp", sl.lane, lambda e, s, sl=sl, u=u: e.dma_start(out=sl.ap.rearrange("p a b -> p (a b)"), in_=wunit_ap(u)).then_inc(s, 16),
                      reads=[unit_res[u]], writes=[sl.res])
                self.nload += 1

            def next(self):
                while self.nload < len(self.seq) and self.nload < self.nuse + NSLOT:
                    self._load()
                sl = self.slots[self.nuse % NSLOT]
                self.nuse += 1
                return sl

        ev_cnt = {"n": 0}

        def rstd_sqrt(col_ap, res, n):
            P.op("dve", lambda e: e.tensor_scalar(out=col_ap, in0=col_ap, scalar1=1.0 / n, scalar2=EPS, op0=ALU.mult, op1=ALU.add), reads=[res], writes=[res])
            P.op("act", lambda e: e.activation(out=col_ap, in_=col_ap, func=AF.Sqrt), reads=[res], writes=[res])
            P.op("dve", lambda e: e.reciprocal(out=col_ap, in_=col_ap), reads=[res], writes=[res])

        def norm_to_xnT(R, src, ts, sidx):
            col = stats[:, sidx:sidx + 1]
            sres = stats_res[sidx % 16]
            junk = R["junk"]
            P.op("act", lambda e: e.activation(out=junk.ap, in_=src.ap, func=AF.Square, accum_out=col), reads=[src.res], writes=[junk.res, sres])
            rstd_sqrt(col, sres, D)
            xnb = R["xnb"][0]
            P.op("dve", lambda e: e.tensor_scalar(out=xnb.ap, in0=src.ap, scalar1=col, scalar2=None, op0=ALU.mult), reads=[src.res, sres], writes=[xnb.res])
            xnT = R["xnT"]
            for hb in range(2):
                b = 4 + hb * 2
                pb = bank_bf(b)
                for c in range(8):
                    kc = hb * 8 + c
                    P.op("pe", lambda e, pb=pb, c=c, kc=kc: e.transpose(pb[:, c * 128:(c + 1) * 128], xnb.ap[:, kc * 128:(kc + 1) * 128], ident.ap),
                         reads=[xnb.res, ident.res], writes=[bank_res[b]])
                dst = xnT.ap[:, hb * 8:(hb + 1) * 8, ts * 128:(ts + 1) * 128]
                srcp = pb.rearrange("p (a b) -> p a b", a=8)
                eng = "dve" if hb == 0 else "act"
                if eng == "dve":
                    P.op("dve", lambda e, dst=dst, srcp=srcp: e.tensor_copy(out=dst, in_=srcp), reads=[bank_res[b]], writes=[xnT.res])
                else:
                    P.op("act", lambda e, dst=dst, srcp=srcp: e.activation(out=dst, in_=srcp, func=AF.Copy), reads=[bank_res[b]], writes=[xnT.res])

        def acc_view(setidx, ts):
            b = 4 + 2 * setidx + ts // 2
            return b, psb[:, b, (ts % 2) * 256:(ts % 2) * 256 + 256]

        def tokmajor_stage(R, ws, lhs_buf, nchunks, ndc, evac):
            nfu = (nchunks + 15) // 16
            for dc in range(ndc):
                setidx = dc % 2
                for fu in range(nfu):
                    nf = min(16, nchunks - fu * 16)
                    sl = ws.next()
                    for ts in range(4):
                        b, acc = acc_view(setidx, ts)
                        for fl in range(nf):
                            f = fu * 16 + fl
                            first = (fu == 0 and fl == 0 and ts % 2 == 0)
                            last = (f == nchunks - 1)
                            P.op("pe", lambda e, acc=acc, f=f, fl=fl, ts=ts, sl=sl, first=first, last=last: e.matmul(
                                acc, lhs_buf.ap[:, f, ts * 128:(ts + 1) * 128], sl.ap[:, fl, :], start=first, stop=last, skip_group_check=True),
                                reads=[lhs_buf.res, sl.res], writes=[bank_res[b]])
                for ts in range(4):
                    b, acc = acc_view(setidx, ts)
                    evac(dc, ts, acc, b)

        def evac_to_ffo(R):
            ssd = R["ssd"]

            def evac(dc, ts, acc, b):
                ffo = R["ffo"][ts]
                fsl = ffo.ap[:, dc * 256:(dc + 1) * 256]
                if ts < 2:
                    P.op("dve", lambda e: e.tensor_copy(out=fsl, in_=acc), reads=[bank_res[b]], writes=[ffo.res])
                else:
                    P.op("act", lambda e: e.activation(out=fsl, in_=acc, func=AF.Copy), reads=[bank_res[b]], writes=[ffo.res])
                jk = R["junk"]
                P.op("act", lambda e: e.activation(out=jk.ap[:, 0:256], in_=fsl, func=AF.Square, accum_out=ssd.ap[:, ts * 8 + dc:ts * 8 + dc + 1]),
                     reads=[ffo.res], writes=[jk.res, ssd.res])
            return evac

        def residual_epilogue(R, ts, src_ap_dram, src_res, gpost, dst_ap_dram, dst_res, sidx, after=None):
            ssd = R["ssd"]
            col = stats[:, sidx:sidx + 1]
            sres = stats_res[sidx % 16]
            P.op("dve", lambda e: e.reduce_sum(out=col, in_=ssd.ap[:, ts * 8:(ts + 1) * 8], axis=AX.X), reads=[ssd.res], writes=[sres])
            rstd_sqrt(col, sres, D)
            xb = R["xres"][ts % 2]
            P.dma("pool", xb.lane, lambda e, s: e.dma_start(out=xb.ap, in_=src_ap_dram).then_inc(s, 16), reads=[src_res], writes=[xb.res])
            ffo = R["ffo"][ts]
            P.op("dve", lambda e: e.scalar_tensor_tensor(out=ffo.ap, in0=ffo.ap, scalar=col, in1=gpost.ap, op0=ALU.mult, op1=ALU.mult),
                 reads=[ffo.res, sres, gpost.res], writes=[ffo.res])
            P.op("pool", lambda e: e.tensor_tensor(out=xb.ap, in0=xb.ap, in1=ffo.ap, op=ALU.add), reads=[xb.res, ffo.res], writes=[xb.res])
            P.dma("pool", xb.lane, lambda e, s: e.dma_start(out=dst_ap_dram, in_=xb.ap).then_inc(s, 16), reads=[xb.res], writes=[dst_res])
            if after is not None:
                after(ts, xb)

        def ffn_stage1(R, ws):
            xnT = R["xnT"]
            actT = R["actT"]
            for j in range(NFC):
                sl = ws.next()
                setidx = j % 2
                bg, bu = 2 * setidx, 2 * setidx + 1
                for kc in range(NKC):
                    P.op("pe", lambda e, kc=kc, sl=sl, bg=bg: e.matmul(bank(bg), sl.ap[:, kc, 0:128], xnT.ap[:, kc, :], start=(kc == 0), stop=(kc == NKC - 1)),
                         reads=[sl.res, xnT.res], writes=[bank_res[bg]])
                for kc in range(NKC):
                    P.op("pe", lambda e, kc=kc, sl=sl, bu=bu: e.matmul(bank(bu), sl.ap[:, kc, 128:256], xnT.ap[:, kc, :], start=(kc == 0), stop=(kc == NKC - 1)),
                         reads=[sl.res, xnT.res], writes=[bank_res[bu]])
                sg = R["sg"][setidx]
                P.op("act", lambda e, sg=sg, bg=bg: e.activation(out=sg.ap, in_=bank(bg), func=AF.Silu), reads=[bank_res[bg]], writes=[sg.res])
                P.op("dve", lambda e, sg=sg, bu=bu, j=j: e.tensor_tensor(out=actT.ap[:, j, :], in0=sg.ap, in1=bank(bu), op=ALU.mult),
                     reads=[sg.res, bank_res[bu]], writes=[actT.res])

        def load_gpost(buf, gi, half):
            P.dma("pool", buf.lane, lambda e, s: e.dma_start(out=buf.ap, in_=DAP(g_post_d[gi], 0, [[0, 128], [1, D]])).then_inc(s, 16), writes=[buf.res])
            if half:
                P.op("dve", lambda e: e.tensor_scalar(out=buf.ap, in0=buf.ap, scalar1=0.5, scalar2=None, op0=ALU.mult), reads=[buf.res], writes=[buf.res])

        def rows(t, r0, n=128):
            return DAP(t, r0 * D, [[D, n], [1, D]])

        hres = [Res() for _ in range(NBLK)]
        xin_res = Res()
        yres = Res()

        if "A" in phases:
            R = setup_ffn_region()
            seqA = []
            for t in range(NT):
                seqA += [U_GU1 + j for j in range(44)] + [U_D1 + j for j in range(24)] + [U_INF + j for j in range(13)] + [U_INT + j for j in range(5)]
            ws = WStream(R, seqA)
            load_gpost(gpostA, 0, True)
            P.op("dve", lambda e: e.memset(R["ssd"].ap, 0.0), writes=[R["ssd"].res])
            sctr = [0]

            def next_sidx():
                sctr[0] += 1
                return sctr[0] % 16

            for t in range(NT if ntiles is None else ntiles):
                tok0 = t * TT
                for ts in range(4):
                    xb = R["xres"][ts % 2]
                    r0 = tok0 + ts * 128
                    P.dma("pool", xb.lane, lambda e, s, xb=xb, r0=r0: e.dma_start(out=xb.ap, in_=rows(x_d, r0)).then_inc(s, 16), reads=[xin_res], writes=[xb.res])
                    norm_to_xnT(R, xb, ts, next_sidx())
                if stop == "prologue":
                    break
                ffn_stage1(R, ws)
                if stop == "stage1":
                    break
                tokmajor_stage(R, ws, R["actT"], NFC, 8, evac_to_ffo(R))
                if stop == "stage2":
                    break

                def after(ts, hb):
                    norm_to_xnT(R, hb, ts, next_sidx())

                for ts in range(4):
                    r0 = tok0 + ts * 128
                    residual_epilogue(R, ts, rows(x_d, r0), xin_res, gpostA, rows(hscr, r0), hres[r0 // 128], next_sidx(), after=after)
                if stop == "epi":
                    break
                xnT = R["xnT"]
                scnt = 0
                for u in range(13):
                    sl = ws.next()
                    for c2 in range(2):
                        ci = 2 * u + c2
                        b = ci % 4
                        for kc in range(NKC):
                            P.op("pe", lambda e, kc=kc, sl=sl, b=b, c2=c2: e.matmul(bank(b), sl.ap[:, kc, c2 * 128:(c2 + 1) * 128], xnT.ap[:, kc, :],
                                                                                   start=(kc == 0), stop=(kc == NKC - 1)),
                                 reads=[sl.res, xnT.res], writes=[bank_res[b]])
                        sb_ = R["stage"][ci % 4]
                        if ci < 8:
                            scale, dst = 0.125, DAP(qda, ci * 128 * TOK + tok0, [[TOK, 128], [1, TT]])
                        elif ci < 16:
                            scale, dst = 1.0, DAP(kda, (ci - 8) * 128 * TOK + tok0, [[TOK, 128], [1, TT]])
                        elif ci < 24:
                            j = ci - 16
                            pi, g = j // 4, j % 4
                            scale = 0.125
                            dst = DAP(qsw, pi * 128 * NBLK * 512 + (tok0 // 128) * 512 + g * 128, [[NBLK * 512, 128], [512, 4], [1, 128]])
                        else:
                            scale, dst = 1.0, DAP(ksw, (ci - 24) * 128 * TOK + tok0, [[TOK, 128], [1, TT]])
                        if ci % 2 == 0:
                            P.op("act", lambda e, sb_=sb_, b=b, scale=scale: e.activation(out=sb_.ap, in_=bank(b), func=AF.Copy, scale=scale),
                                 reads=[bank_res[b]], writes=[sb_.res])
                        else:
                            P.op("dve", lambda e, sb_=sb_, b=b, scale=scale: e.tensor_scalar(out=sb_.ap, in0=bank(b), scalar1=scale, scalar2=None, op0=ALU.mult),
                                 reads=[bank_res[b]], writes=[sb_.res])
                        if 16 <= ci < 24:
                            src = sb_.ap.rearrange("p (a b) -> p a b", a=4)
                        else:
                            src = sb_.ap
                        P.dma("pool", sb_.lane, lambda e, s, dst=dst, src=src: e.dma_start(out=dst, in_=src).then_inc(s, 16), reads=[sb_.res])
                if stop == "projF":
                    break
                for u in range(5):
                    sl = ws.next()
                    setidx = u % 2
                    for ts in range(4):
                        b, acc = acc_view(setidx, ts)
                        for kc in range(NKC):
                            first = (kc == 0 and ts % 2 == 0)
                            P.op("pe", lambda e, acc=acc, kc=kc, ts=ts, sl=sl, first=first: e.matmul(
                                acc, xnT.ap[:, kc, ts * 128:(ts + 1) * 128], sl.ap[:, kc, :], start=first, stop=(kc == NKC - 1), skip_group_check=True),
                                reads=[xnT.res, sl.res], writes=[bank_res[b]])
                    for ts in range(4):
                        b, acc = acc_view(setidx, ts)
                        sb_ = R["stage"][ts]
                        r0 = tok0 + ts * 128
                        vv = sb_.ap[:, 0:256]
                        if ts < 2:
                            P.op("act", lambda e, vv=vv, acc=acc: e.activation(out=vv, in_=acc, func=AF.Copy), reads=[bank_res[b]], writes=[sb_.res])
                        else:
                            P.op("dve", lambda e, vv=vv, acc=acc: e.tensor_copy(out=vv, in_=acc), reads=[bank_res[b]], writes=[sb_.res])
                        if u < 4:
                            dst = DAP(vda, (2 * u) * TOK * 128 + r0 * 128, [[128, 128], [TOK * 128, 2], [1, 128]])
                            src = vv.rearrange("p (a b) -> p a b", a=2)
                        else:
                            dst = DAP(vsw, r0 * 64, [[64, 128], [TOK * 64, 4], [1, 64]])
                            src = vv.rearrange("p (a b) -> p a b", a=4)
                        P.dma("pool", sb_.lane, lambda e, s, dst=dst, src=src: e.dma_start(out=dst, in_=src).then_inc(s, 16), reads=[sb_.res])
            P.barrier()

        if "B" in phases:
            A.off = persist_mark
            rbx_f = Buf(f32v(24), "rbx")
            rbx = Buf(bfv(24), None)
            P.op("dve", lambda e: e.memset(rbx_f.ap[0:64, :], 1.0), writes=[rbx_f.res])
            P.dma("pool", "rbx", lambda e, s: e.dma_start(out=rbx_f.ap[0:32, :], in_=DAP(relb_d, 0, [[24, 32], [1, 24]])).then_inc(s, 16), writes=[rbx_f.res])
            P.op("dve", lambda e: e.tensor_copy(out=rbx.ap[0:64, :], in_=rbx_f.ap[0:64, :]), reads=[rbx_f.res], writes=[rbx.res])
            oht = Buf(bfv(LDA + LSW), "oht")
            P.dma("pool", "oht", lambda e, s: e.dma_start(out=oht.ap[0:33, 0:LDA], in_=DAP(ohda_d, 0, [[LDA, 33], [1, LDA]])).then_inc(s, 16), writes=[oht.res])
            P.dma("pool", "oht", lambda e, s: e.dma_start(out=oht.ap[0:33, LDA:LDA + LSW], in_=DAP(ohsw_d, 0, [[LSW, 33], [1, LSW]])).then_inc(s, 16), writes=[oht.res])
            tabs = Buf(bfv(LDA + LSW), "tabs")
            col0 = 0
            for (c0, n) in ((0, 512), (512, 512), (1024, 256), (LDA, 512)):
                b = (c0 // 512) % 4
                P.op("pe", lambda e, c0=c0, n=n, b=b: e.matmul(psb[0:24, b, 0:n], rbx.ap[0:33, :], oht.ap[0:33, c0:c0 + n], start=True, stop=True),
                     reads=[rbx.res, oht.res], writes=[bank_res[b]])
                P.op("dve", lambda e, c0=c0, n=n, b=b: e.tensor_copy(out=tabs.ap[0:24, c0:c0 + n], in_=psb[0:24, b, 0:n]), reads=[bank_res[b]], writes=[tabs.res])
            tab_res = Res()
            P.dma("pool", "tabs", lambda e, s: e.dma_start(out=DAP(tabscr, 0, [[2048, 24], [1, LDA + LSW]]), in_=tabs.ap[0:24, :]).then_inc(s, 16),
                  reads=[tabs.res], writes=[tab_res])
            strip_da = Buf(bfv(8 * 1152).rearrange("p (a b) -> p a b", a=8), None)
            strip_sw = Buf(bfv(16 * 384).rearrange("p (a b) -> p a b", a=16), None)
            hank = [Buf(bfv(1152), "hank%d" % i) for i in range(2)]
            for h in range(8):
                hk = hank[h % 2]
                P.dma("pool", hk.lane, lambda e, s, hk=hk, h=h: e.dma_start(out=hk.ap, in_=DAP(tabscr, h * 2048, [[1, 128], [1, 1152]])).then_inc(s, 16),
                      reads=[tab_res], writes=[hk.res])
                rv = hk.ap
                rev = bass.AP(rv.tensor, rv.offset + 1151, [list(rv.ap[0]), [-1, 1152]])
                P.op("dve", lambda e, h=h, rev=rev: e.tensor_copy(out=strip_da.ap[:, h, :], in_=rev), reads=[hk.res], writes=[strip_da.res])
            for hq in range(16):
                hk = hank[hq % 2]
                P.dma("pool", hk.lane, lambda e, s, hk=hk, hq=hq: e.dma_start(out=hk.ap[:, 0:384], in_=DAP(tabscr, (8 + hq) * 2048 + LDA, [[1, 128], [1, 384]])).then_inc(s, 16),
                      reads=[tab_res], writes=[hk.res])
                rv = hk.ap
                rev = bass.AP(rv.tensor, rv.offset + 383, [list(rv.ap[0]), [-1, 384]])
                P.op("dve", lambda e, hq=hq, rev=rev: e.tensor_copy(out=strip_sw.ap[:, hq, :], in_=rev), reads=[hk.res], writes=[strip_sw.res])

            SMAX = max(seqs)
            NBMAX = SMAX // 128
            VP = 130
            qT = [Buf(bfv(SMAX), "qT%d" % i) for i in range(2)]
            kT = [Buf(bfv(SMAX), "kT%d" % i) for i in range(2)]
            vv_ = [Buf(bfv(NBMAX * VP).rearrange("p (a b) -> p a b", a=NBMAX), "v%d" % i) for i in range(2)]
            for i in range(2):
                P.op("pool", lambda e, i=i: e.memset(vv_[i].ap[:, :, 128:130], 1.0), writes=[vv_[i].res])
            eT = [Buf(bfv(1024), None) for i in range(3)]
            accs = Buf(f32v(3 * 512), None)
            o1 = Buf(f32v(128), None)
            o2 = Buf(f32v(128), None)
            ob16 = [Buf(bfv(128), None) for i in range(2)]
            junkB = Buf(bfv(128), None)
            mstage = [Buf(bfv(512), "mst%d" % i) for i in range(2)]
            rz = f32v(8)
            rz_res = Res()
            it = {"qk": 0, "e": 0, "ms": 0, "ob": 0}

            def da_head(si, h, hidx):
                S = seqs[si]
                t0 = seq_off[si]
                nb = S // 128
                q_, k_, v_ = qT[hidx % 2], kT[hidx % 2], vv_[hidx % 2]
                P.dma("sp", q_.lane, lambda e, s: e.dma_start(out=q_.ap[:, 0:S], in_=DAP(qda, h * 128 * TOK + t0, [[TOK, 128], [1, S]])).then_inc(s, 16), writes=[q_.res])
                P.dma("sp", k_.lane, lambda e, s: e.dma_start(out=k_.ap[:, 0:S], in_=DAP(kda, h * 128 * TOK + t0, [[TOK, 128], [1, S]])).then_inc(s, 16), writes=[k_.res])
                P.dma("sp", v_.lane, lambda e, s: e.dma_start(out=v_.ap[:, 0:nb, 0:128], in_=DAP(vda, h * TOK * 128 + t0 * 128, [[128, 128], [128 * 128, nb], [1, 128]])).then_inc(s, 16),
                      writes=[v_.res])
                clo = small.ap[:, 32 + h:33 + h]
                chi = small.ap[:, 40 + h:41 + h]
                for qc in range(S // 512):
                    for kb in range(nb):
                        Dd = kb - 4 * qc
                        near = (-1 <= Dd <= 4)
                        setidx = it["qk"] % 2
                        it["qk"] += 1
                        bA, bB = 2 * setidx, 2 * setidx + 1
                        for m, b in ((0, bA), (1, bB)):
                            P.op("pe", lambda e, m=m, b=b, kb=kb, qc=qc: e.matmul(
                                bank(b), k_.ap[m * 64:(m + 1) * 64, kb * 128:(kb + 1) * 128], q_.ap[m * 64:(m + 1) * 64, qc * 512:(qc + 1) * 512],
                                start=True, stop=(not near)), reads=[k_.res, q_.res], writes=[bank_res[b]])
                        if near:
                            m0 = 4 - Dd
                            for b in (bA, bB):
                                P.op("pe", lambda e, b=b, m0=m0: e.matmul(bank(b), ident.ap, strip_da.ap[:, h, m0 * 128:m0 * 128 + 512], start=False, stop=True),
                                     reads=[ident.res, strip_da.res], writes=[bank_res[b]])
                        et = eT[it["e"] % 3]
                        it["e"] += 1
                        pair = psb[:, bA:bA + 2, :].rearrange("p a b -> p (a b)")
                        if near:
                            P.op("act", lambda e, et=et, pair=pair: e.activation(out=et.ap, in_=pair, func=AF.Exp),
                                 reads=[bank_res[bA], bank_res[bB]], writes=[et.res])
                        else:
                            cb = clo if Dd < 0 else chi
                            P.op("act", lambda e, et=et, pair=pair, cb=cb: e.activation(out=et.ap, in_=pair, func=AF.Exp, bias=cb),
                                 reads=[bank_res[bA], bank_res[bB], small.res], writes=[et.res])
                        for m in range(2):
                            for qs in range(4):
                                a = m * 4 + qs
                                b = 4 + a // 3
                                c0 = (a % 3) * 129
                                first = (kb == 0 and a % 3 == 0)
                                P.op("pe", lambda e, et=et, m=m, qs=qs, b=b, c0=c0, kb=kb, first=first: e.matmul(
                                    psb[:, b, c0:c0 + 129], et.ap[:, m * 512 + qs * 128:m * 512 + (qs + 1) * 128], v_.ap[:, kb, 0:129],
                                    start=first, stop=(kb == nb - 1), skip_group_check=True), reads=[et.res, v_.res], writes=[bank_res[b]])
                    for bi in range(3):
                        eng = "dve" if bi != 1 else "act"
                        if eng == "dve":
                            P.op("dve", lambda e, bi=bi: e.tensor_copy(out=accs.ap[:, bi * 512:bi * 512 + 387], in_=psb[:, 4 + bi, 0:387]),
                                 reads=[bank_res[4 + bi]], writes=[accs.res])
                        else:
                            P.op("act", lambda e, bi=bi: e.activation(out=accs.ap[:, bi * 512:bi * 512 + 387], in_=psb[:, 4 + bi, 0:387], func=AF.Copy),
                                 reads=[bank_res[4 + bi]], writes=[accs.res])
                    ms = mstage[it["ms"] % 2]
                    it["ms"] += 1
                    for qs in range(4):
                        a1, a2 = qs, 4 + qs
                        v1 = accs.ap[:, (a1 // 3) * 512 + (a1 % 3) * 129:(a1 // 3) * 512 + (a1 % 3) * 129 + 129]
                        v2 = accs.ap[:, (a2 // 3) * 512 + (a2 % 3) * 129:(a2 // 3) * 512 + (a2 % 3) * 129 + 129]
                        P.op("dve", lambda e, v1=v1: e.reciprocal(out=rz[:, 0:1], in_=v1[:, 128:129]), reads=[accs.res], writes=[rz_res])
                        P.op("dve", lambda e, v2=v2: e.reciprocal(out=rz[:, 1:2], in_=v2[:, 128:129]), reads=[accs.res, rz_res], writes=[rz_res])
                        P.op("dve", lambda e: e.tensor_tensor(out=rz[:, 1:2], in0=rz[:, 1:2], in1=neglam, op=ALU.mult), reads=[rz_res, small.res], writes=[rz_res])
                        P.op("dve", lambda e, v1=v1: e.tensor_scalar(out=o1.ap, in0=v1[:, 0:128], scalar1=rz[:, 0:1], scalar2=None, op0=ALU.mult),
                             reads=[accs.res, rz_res], writes=[o1.res])
                        P.op("dve", lambda e, v2=v2: e.scalar_tensor_tensor(out=o2.ap, in0=v2[:, 0:128], scalar=rz[:, 1:2], in1=o1.ap, op0=ALU.mult, op1=ALU.add),
                             reads=[accs.res, rz_res, o1.res], writes=[o2.res])
                        P.op("act", lambda e: e.activation(out=junkB.ap, in_=o2.ap, func=AF.Square, accum_out=rz[:, 2:3]), reads=[o2.res, rz_res], writes=[junkB.res, rz_res])
                        P.op("dve", lambda e: e.tensor_scalar(out=rz[:, 2:3], in0=rz[:, 2:3], scalar1=1.0 / 128, scalar2=EPS, op0=ALU.mult, op1=ALU.add),
                             reads=[rz_res], writes=[rz_res])
                        P.op("act", lambda e: e.activation(out=rz[:, 2:3], in_=rz[:, 2:3], func=AF.Ln), reads=[rz_res], writes=[rz_res])
                        P.op("act", lambda e: e.activation(out=rz[:, 2:3], in_=rz[:, 2:3], func=AF.Exp, scale=-0.5), reads=[rz_res], writes=[rz_res])
                        ob = ob16[it["ob"] % 2]
                        it["ob"] += 1
                        P.op("dve", lambda e, ob=ob: e.scalar_tensor_tensor(out=ob.ap, in0=o2.ap, scalar=rz[:, 2:3], in1=gsub.ap, op0=ALU.mult, op1=ALU.mult),
                             reads=[o2.res, rz_res, gsub.res], writes=[ob.res])
                        P.op("pe", lambda e, ob=ob, qs=qs: e.transpose(bank_bf(7)[:, qs * 128:(qs + 1) * 128], ob.ap, ident.ap),
                             reads=[ob.res, ident.res], writes=[bank_res[7]])
                    P.op("dve", lambda e, ms=ms: e.tensor_copy(out=ms.ap, in_=bank_bf(7)[:, 0:512]), reads=[bank_res[7]], writes=[ms.res])
                    P.dma("pool", ms.lane, lambda e, s, ms=ms, qc=qc: e.dma_start(out=DAP(mixT, h * 128 * TOK + t0 + qc * 512, [[TOK, 128], [1, 512]]), in_=ms.ap).then_inc(s, 16),
                          reads=[ms.res])

            SEGB = 8
            qs_t = [Buf(bfv(SEGB * 512).rearrange("p (a g q) -> p a g q", a=SEGB, g=4), "qs%d" % i) for i in range(2)]
            ks_t = [Buf(bfv((SEGB + 2) * 128), "ks%d" % i) for i in range(2)]
            VS = 66
            vs_t = [Buf(bfv((SEGB + 2) * 2 * VS).rearrange("p (a h d) -> p a h d", a=SEGB + 2, h=2), "vs%d" % i) for i in range(2)]
            for i in range(2):
                P.op("pool", lambda e, i=i: e.memset(vs_t[i].ap[:, :, :, 64:66], 1.0), writes=[vs_t[i].res])
            eS = [Buf(bfv(512), None) for i in range(3)]
            accS = Buf(f32v(4 * VS), None)
            zz = f32v(8)
            zz_res = Res()
            osw = [Buf(bfv(512).rearrange("p (g d) -> p g d", g=4), None) for i in range(2)]
            sit = {"b": 0, "e": 0, "acc": 0, "o": 0, "ms": 0}

            def sw_seg(si, pi, seg, sidx):
                S = seqs[si]
                t0 = seq_off[si]
                nb = S // 128
                i0 = seg * SEGB
                i1 = min(nb, i0 + SEGB)
                kb0 = max(0, i0 - 1)
                kb1 = min(nb, i1 + 1)
                nkb = kb1 - kb0
                nq = i1 - i0
                q_, k_, v_ = qs_t[sidx % 2], ks_t[sidx % 2], vs_t[sidx % 2]
                blk0 = t0 // 128 + i0
                P.dma("sp", q_.lane, lambda e, s: e.dma_start(out=q_.ap[:, 0:nq].rearrange("p a g q -> p (a g q)"),
                                                              in_=DAP(qsw, pi * 128 * NBLK * 512 + blk0 * 512, [[NBLK * 512, 128], [1, nq * 512]])).then_inc(s, 16), writes=[q_.res])
                P.dma("sp", k_.lane, lambda e, s: e.dma_start(out=k_.ap[:, 0:nkb * 128], in_=DAP(ksw, pi * 128 * TOK + t0 + kb0 * 128, [[TOK, 128], [1, nkb * 128]])).then_inc(s, 16),
                      writes=[k_.res])
                for half in range(2):
                    kvh = 2 * pi + half
                    P.dma("sp", v_.lane, lambda e, s, half=half, kvh=kvh: e.dma_start(
                        out=v_.ap[:, 0:nkb, half, 0:64], in_=DAP(vsw, kvh * TOK * 64 + (t0 + kb0 * 128) * 64, [[64, 128], [128 * 64, nkb], [1, 64]])).then_inc(s, 16),
                        writes=[v_.res])
                for i in range(i0, i1):
                    ow = osw[sit["o"] % 2]
                    sit["o"] += 1
                    for half in range(2):
                        hq0 = (2 * pi + half) * 4
                        kbs = [kb for kb in (i - 1, i, i + 1) if 0 <= kb < nb]
                        ab = 4 + sit["acc"] % 2
                        sit["acc"] += 1
                        for idx, kb in enumerate(kbs):
                            rr = i - kb + 1
                            b = sit["b"] % 4
                            sit["b"] += 1
                            P.op("pe", lambda e, b=b, kb=kb, i=i, half=half: e.matmul(
                                bank(b), k_.ap[half * 64:(half + 1) * 64, (kb - kb0) * 128:(kb - kb0 + 1) * 128], q_.ap[half * 64:(half + 1) * 64, i - i0],
                                start=True, stop=False), reads=[k_.res, q_.res], writes=[bank_res[b]])
                            P.op("pe", lambda e, b=b, rr=rr, hq0=hq0: e.matmul(bank(b), ident.ap, strip_sw.ap[:, hq0:hq0 + 4, rr * 128:(rr + 1) * 128], start=False, stop=True),
                                 reads=[ident.res, strip_sw.res], writes=[bank_res[b]])
                            et = eS[sit["e"] % 3]
                            sit["e"] += 1
                            P.op("act", lambda e, et=et, b=b: e.activation(out=et.ap, in_=bank(b), func=AF.Exp), reads=[bank_res[b]], writes=[et.res])
                            for g in range(4):
                                P.op("pe", lambda e, et=et, g=g, ab=ab, kb=kb, half=half, idx=idx, last=(idx == len(kbs) - 1): e.matmul(
                                    psb[:, ab, g * VS:g * VS + 65], et.ap[:, g * 128:(g + 1) * 128], v_.ap[:, kb - kb0, half, 0:65],
                                    start=(idx == 0 and g == 0), stop=last, skip_group_check=True), reads=[et.res, v_.res], writes=[bank_res[ab]])
                        P.op("dve", lambda e, ab=ab: e.tensor_copy(out=accS.ap, in_=psb[:, ab, 0:4 * VS]), reads=[bank_res[ab]], writes=[accS.res])
                        a3 = accS.ap.rearrange("p (g d) -> p g d", g=4)
                        P.op("dve", lambda e, a3=a3, hq0=hq0: e.tensor_tensor(out=zz[:, 0:4], in0=a3[:, :, 64], in1=small.ap[:, 8 + hq0:12 + hq0], op=ALU.add),
                             reads=[accS.res, small.res], writes=[zz_res])
                        P.op("dve", lambda e: e.reciprocal(out=zz[:, 0:4], in_=zz[:, 0:4]), reads=[zz_res], writes=[zz_res])
                        zv = zz[:, 0:4]
                        zb = bass.AP(zv.tensor, zv.offset, [list(zv.ap[0]), [1, 4], [0, 64]])
                        P.op("dve", lambda e, a3=a3, ow=ow, half=half, zb=zb: e.tensor_tensor(out=ow.ap[:, :, half * 64:(half + 1) * 64], in0=a3[:, :, 0:64], in1=zb, op=ALU.mult),
                             reads=[accS.res, zz_res], writes=[ow.res])
                    for g in range(4):
                        P.op("pe", lambda e, ow=ow, g=g: e.transpose(bank_bf(7)[:, g * 128:(g + 1) * 128], ow.ap[:, g, :], ident.ap),
                             reads=[ow.res, ident.res], writes=[bank_res[7]])
                    ms = mstage[sit["ms"] % 2]
                    sit["ms"] += 1
                    P.op("dve", lambda e, ms=ms: e.tensor_copy(out=ms.ap, in_=bank_bf(7)[:, 0:512]), reads=[bank_res[7]], writes=[ms.res])
                    P.dma("pool", ms.lane, lambda e, s, ms=ms, i=i: e.dma_start(
                        out=DAP(mixT, (8 + pi * 4) * 128 * TOK + t0 + i * 128, [[TOK, 128], [128 * TOK, 4], [1, 128]]),
                        in_=ms.ap.rearrange("p (g q) -> p g q", g=4)).then_inc(s, 16), reads=[ms.res])

            hidx = 0
            sidx = 0
            for si in range(len(seqs)):
                for h in range(8):
                    da_head(si, h, hidx)
                    hidx += 1
                nseg = (seqs[si] // 128 + SEGB - 1) // SEGB
                for pi in range(2):
                    for seg in range(nseg):
                        sw_seg(si, pi, seg, sidx)
                        sidx += 1
            P.barrier()

        if "C" in phases:
            R = setup_ffn_region()
            seqC = []
            for t in range(NT):
                seqC += [U_OUT + j for j in range(8)] + [U_GU2 + j for j in range(44)] + [U_D2 + j for j in range(24)]
            ws = WStream(R, seqC)
            load_gpost(gpostA, 1, False)
            load_gpost(gpostB, 2, True)
            P.op("dve", lambda e: e.memset(R["ssd"].ap, 0.0), writes=[R["ssd"].res])
            sctr = [0]

            def next_sidx2():
                sctr[0] += 1
                return sctr[0] % 16

            for t in range(NT):
                tok0 = t * TT
                xnT = R["xnT"]
                P.dma("pool", xnT.lane, lambda e, s, tok0=tok0: e.dma_start(out=xnT.ap, in_=DAP(mixT, tok0, [[TOK, 128], [128 * TOK, 16], [1, TT]])).then_inc(s, 16),
                      writes=[xnT.res])
                tokmajor_stage(R, ws, xnT, 16, 8, evac_to_ffo(R))

                def after2(ts, hb):
                    norm_to_xnT(R, hb, ts, next_sidx2())

                for ts in range(4):
                    r0 = tok0 + ts * 128
                    residual_epilogue(R, ts, rows(hscr, r0), hres[r0 // 128], gpostA, rows(hscr, r0), hres[r0 // 128], next_sidx2(), after=after2)
                ffn_stage1(R, ws)
                tokmajor_stage(R, ws, R["actT"], NFC, 8, evac_to_ffo(R))
                for ts in range(4):
                    r0 = tok0 + ts * 128
                    residual_epilogue(R, ts, rows(hscr, r0), hres[r0 // 128], gpostB, rows(y_d, r0), yres, next_sidx2())

        P.emit_all(nc)
    return nc


_CACHE = {}


def _core_inputs(inputs, c, oh_da, oh_sw):
    xp = np.asarray(inputs["x_prompt"])
    xs = np.asarray(inputs["x_sample"])
    x = np.concatenate([xp[2 * c].reshape(-1, D), xp[2 * c + 1].reshape(-1, D), xs[c].reshape(-1, D)], axis=0)
    m = {"x": np.ascontiguousarray(x, dtype=np.float32)}
    for k in ("w_ffn1_gu", "w_ffn2_gu", "w_ffn1_down", "w_ffn2_down", "w_in", "w_out"):
        m[k] = np.ascontiguousarray(np.asarray(inputs[k])[0], dtype=np.float32)
    m["rel_bias"] = np.ascontiguousarray(np.asarray(inputs["rel_bias"]), dtype=np.float32)
    for k in ("g_ffn1_pre", "g_mix_pre", "g_ffn2_pre", "g_ffn1_post", "g_mix_post", "g_ffn2_post",
              "lambda_q1", "lambda_k1", "lambda_q2", "lambda_k2", "g_diff_subln", "sink_logit"):
        m[k] = np.ascontiguousarray(np.asarray(inputs[k]), dtype=np.float32).reshape(1, -1)
    m["oh_da"] = oh_da
    m["oh_sw"] = oh_sw
    return m


def kernel(**inputs):
    seqs = [2048, 2048, 4096]
    if "nc" not in _CACHE:
        _CACHE["nc"] = build(seqs)
    nc = _CACHE["nc"]
    oh_da, oh_sw = _onehot_tables()
    in_maps = [_core_inputs(inputs, c, oh_da, oh_sw) for c in range(8)]
    res = run_bass_kernel_spmd(nc, in_maps, core_ids=list(range(8)))
    yp = np.empty((16, 2048, D), np.float32)
    ys = np.empty((8, 4096, D), np.float32)
    for c in range(8):
        y = np.asarray(res.results[c]["y"])
        yp[2 * c] = y[0:2048]
        yp[2 * c + 1] = y[2048:4096]
        ys[c] = y[4096:8192]
    return (yp, ys)
```

```python
import math
import contextlib
import numpy as np
import ml_dtypes
import concourse.bass as bass
import concourse.mybir as mybir
from concourse.bass_utils import run_bass_kernel_spmd

F32 = mybir.dt.float32
BF16 = mybir.dt.bfloat16
AF = mybir.ActivationFunctionType
ALU = mybir.AluOpType
AX = mybir.AxisListType

D = 2048
DFF = 5632
DIN = 4608
NKC = 16
NFC = 44
EPS = 1e-6
LAMBDA_INIT = 0.8 - 0.6 * math.exp(-0.3 * 0)
TT = 512
NSLOT = 4
MASKV = -30000.0

U_GU1, U_D1, U_INF, U_INT, U_OUT, U_GU2, U_D2 = 0, 44, 68, 81, 86, 94, 138
NUNIT = 162
LDA = 1280
LSW = 512


class Res:
    __slots__ = ("w", "rs")

    def __init__(self):
        self.w = None
        self.rs = []


class Op:
    __slots__ = ("eng", "emit", "deps", "flag", "cnt", "lane", "lval")

    def __init__(self, eng, emit):
        self.eng = eng
        self.emit = emit
        self.deps = []
        self.flag = False
        self.cnt = None
        self.lane = None
        self.lval = None


class Prog:
    ENGS = ("pe", "act", "dve", "pool", "sp")

    def __init__(self):
        self.ops = {e: [] for e in self.ENGS}
        self.lanes = {}
        self.last_lane_op = {}

    def _add(self, op, reads, writes):
        deps = []
        for r in reads:
            if r.w is not None:
                deps.append(r.w)
        for w in writes:
            if w.w is not None:
                deps.append(w.w)
            deps.extend(w.rs)
        seen = set()
        for d in deps:
            if d is op or id(d) in seen:
                continue
            seen.add(id(d))
            if d.lane is None and d.eng == "pe" and op.eng == "pe" and op.lane is None:
                continue
            op.deps.append(d)
            if d.lane is None:
                d.flag = True
        for w in writes:
            w.w = op
            w.rs = []
        for r in reads:
            r.rs.append(op)
        self.ops[op.eng].append(op)
        return op

    def op(self, eng, emit, reads=(), writes=()):
        return self._add(Op(eng, emit), reads, writes)

    def dma(self, eng, lane, emit, ndma=1, reads=(), writes=()):
        o = Op(eng, emit)
        o.lane = lane
        self.lanes[lane] = self.lanes.get(lane, 0) + 16 * ndma
        o.lval = self.lanes[lane]
        self.last_lane_op[lane] = o
        return self._add(o, reads, writes)

    def barrier(self):
        lasts = []
        for e in self.ENGS:
            for o in reversed(self.ops[e]):
                if o.lane is None and o.emit is not None:
                    lasts.append(o)
                    break
        lane_ops = list(self.last_lane_op.values())
        for e in self.ENGS:
            o = Op(e, None)
            for d in lasts:
                if d.eng != e:
                    o.deps.append(d)
                    d.flag = True
            o.deps.extend(lane_ops)
            self.ops[e].append(o)

    def emit_all(self, nc):
        for e in self.ENGS:
            c = 0
            for o in self.ops[e]:
                if o.lane is None and o.flag:
                    c += 1
                    o.cnt = c
        with contextlib.ExitStack() as st:
            esem = {e: st.enter_context(nc.semaphore("s_" + e)) for e in self.ENGS}
            lsem = {l: st.enter_context(nc.semaphore("l_" + l)) for l in self.lanes}
            block = st.enter_context(nc.Block())

            def run(e, engobj):
                waited = {}
                for o in self.ops[e]:
                    for d in o.deps:
                        if d.lane is None:
                            key = ("e", d.eng)
                            sem = esem[d.eng]
                            val = d.cnt
                        else:
                            key = ("l", d.lane)
                            sem = lsem[d.lane]
                            val = d.lval
                        if waited.get(key, 0) >= val:
                            continue
                        waited[key] = val
                        engobj.wait_ge(sem, val)
                    if o.emit is None:
                        continue
                    if o.lane is None:
                        ins = o.emit(engobj)
                        if o.flag:
                            ins.then_inc(esem[e], 1)
                    else:
                        o.emit(engobj, lsem[o.lane])
                if e == "sp":
                    for l, v in self.lanes.items():
                        if waited.get(("l", l), 0) < v:
                            engobj.wait_ge(lsem[l], v)

            @block.tensor
            def _(t):
                run("pe", t)

            @block.scalar
            def _(s):
                run("act", s)

            @block.vector
            def _(v):
                run("dve", v)

            @block.gpsimd
            def _(g):
                run("pool", g)

            @block.sync
            def _(s):
                run("sp", s)


class Buf:
    __slots__ = ("ap", "res", "lane")

    def __init__(self, ap, lane=None):
        self.ap = ap
        self.res = Res()
        self.lane = lane


def _t5_bucket_np(rp):
    rp = np.asarray(rp, np.int32)
    half = 16
    max_exact = 8
    ret = np.where(rp > 0, half, 0).astype(np.int32)
    n = np.abs(rp)
    nf = np.maximum(n, 1).astype(np.float32)
    large = max_exact + (np.log(nf / np.float32(max_exact)) / np.float32(math.log(128 / max_exact))
                         * np.float32(half - max_exact)).astype(np.int32)
    large = np.minimum(large, half - 1)
    return ret + np.where(n < max_exact, n, large)


def _onehot_tables():
    t_da = np.arange(LDA) - 639
    oh_da = np.zeros((33, LDA), np.float32)
    oh_da[_t5_bucket_np(t_da), np.arange(LDA)] = 1.0
    t_sw = np.arange(LSW) - 255
    oh_sw = np.zeros((33, LSW), np.float32)
    oh_sw[_t5_bucket_np(t_sw), np.arange(LSW)] = 1.0
    oh_sw[32, :] = np.where(np.abs(t_sw) > 128, MASKV, 0.0)
    return oh_da.astype(ml_dtypes.bfloat16), oh_sw.astype(ml_dtypes.bfloat16)


def build(seqs, debug=False, phases="0ABC", stop=None, ntiles=None):
    nc = bass.Bass("TRN2", target_bir_lowering=False)
    TOK = sum(seqs)
    NT = TOK // TT
    NBLK = TOK // 128
    seq_off = [sum(seqs[:i]) for i in range(len(seqs))]
    P = Prog()
    skind = "ExternalOutput" if debug else "Internal"

    def din(name, shape, dt=F32):
        return nc.dram_tensor(name, shape, dt, kind="ExternalInput")

    x_d = din("x", [TOK, D])
    y_d = nc.dram_tensor("y", [TOK, D], F32, kind="ExternalOutput")
    wgu_d = [din("w_ffn1_gu", [D, 2 * DFF]), din("w_ffn2_gu", [D, 2 * DFF])]
    wdn_d = [din("w_ffn1_down", [DFF, D]), din("w_ffn2_down", [DFF, D])]
    win_d = din("w_in", [D, DIN])
    wout_d = din("w_out", [D, D])
    relb_d = din("rel_bias", [32, 24])
    g_pre_d = [din("g_ffn1_pre", [1, D]), din("g_mix_pre", [1, D]), din("g_ffn2_pre", [1, D])]
    g_post_d = [din("g_ffn1_post", [1, D]), din("g_mix_post", [1, D]), din("g_ffn2_post", [1, D])]
    lam_d = [din("lambda_q1", [1, 64]), din("lambda_k1", [1, 64]), din("lambda_q2", [1, 64]), din("lambda_k2", [1, 64])]
    gsub_d = din("g_diff_subln", [1, 128])
    sink_d = din("sink_logit", [1, 16])
    ohda_d = din("oh_da", [33, LDA], BF16)
    ohsw_d = din("oh_sw", [33, LSW], BF16)

    wscr = nc.dram_tensor("wscr", [NUNIT, 128, 4096], BF16, kind=skind)
    hscr = nc.dram_tensor("hscr", [TOK, D], F32, kind=skind)
    qda = nc.dram_tensor("qda", [8, 128, TOK], BF16, kind=skind)
    kda = nc.dram_tensor("kda", [8, 128, TOK], BF16, kind=skind)
    vda = nc.dram_tensor("vda", [8, TOK, 128], BF16, kind=skind)
    qsw = nc.dram_tensor("qsw", [2, 128, NBLK, 4, 128], BF16, kind=skind)
    ksw = nc.dram_tensor("ksw", [2, 128, TOK], BF16, kind=skind)
    vsw = nc.dram_tensor("vsw", [4, TOK, 64], BF16, kind=skind)
    mixT = nc.dram_tensor("mixT", [16, 128, TOK], BF16, kind=skind)
    tabscr = nc.dram_tensor("tabscr", [24, 2048], BF16, kind="Internal")

    def DAP(t, off, pat):
        return bass.AP(t, off, [list(p) for p in pat])

    with contextlib.ExitStack() as st:
        NBYTES = 200 * 1024
        big = st.enter_context(nc.sbuf_tensor("big", [128, NBYTES // 2], BF16))
        psb = st.enter_context(nc.psum_tensor("psb", [128, 8, 512], F32))

        class Arena:
            def __init__(self):
                self.off = 0

            def alloc(self, nbytes, align=64):
                self.off = (self.off + align - 1) // align * align
                o = self.off
                self.off += nbytes
                assert self.off <= NBYTES, ("SBUF overflow", self.off)
                return o

        A = Arena()

        def bfv(nelem):
            o = A.alloc(nelem * 2)
            return big[:, o // 2:o // 2 + nelem]

        def f32v(nelem):
            o = A.alloc(nelem * 4)
            return big[:, o // 2:o // 2 + 2 * nelem].bitcast(F32)

        def bank(b):
            return psb[:, b, :]

        def bank_bf(b):
            return psb[:, b, :].bitcast(BF16)

        bank_res = [Res() for _ in range(8)]

        ident = Buf(bfv(128))
        identf = f32v(128)
        P.op("pool", lambda e: e.memset(identf, 0.0), writes=[ident.res])
        P.op("pool", lambda e: e.affine_select(out=identf, in_=identf, pattern=[[-1, 128]], compare_op=ALU.not_equal,
                                               fill=1.0, base=0, channel_multiplier=1), reads=[ident.res], writes=[ident.res])
        P.op("dve", lambda e: e.tensor_copy(out=ident.ap, in_=identf), reads=[ident.res], writes=[ident.res])

        stats = f32v(256)
        stats_res = [Res() for _ in range(16)]
        gpostA = Buf(f32v(D), "gpa")
        small = Buf(f32v(512), "small")
        gsub = Buf(f32v(128), "gsub")
        lamt = f32v(4 * 64)
        lam_res = Res()
        for i in range(4):
            P.dma("pool", "small", lambda e, s, i=i: e.dma_start(out=lamt[:, i * 64:(i + 1) * 64], in_=DAP(lam_d[i], 0, [[0, 128], [1, 64]])).then_inc(s, 16),
                  writes=[lam_res])
        P.dma("pool", "small", lambda e, s: e.dma_start(out=small.ap[:, 8:24], in_=DAP(sink_d, 0, [[0, 128], [1, 16]])).then_inc(s, 16), writes=[small.res])
        P.dma("pool", "small", lambda e, s: e.dma_start(out=small.ap[:, 32:40], in_=DAP(relb_d, 15 * 24, [[0, 128], [1, 8]])).then_inc(s, 16), writes=[small.res])
        P.dma("pool", "small", lambda e, s: e.dma_start(out=small.ap[:, 40:48], in_=DAP(relb_d, 31 * 24, [[0, 128], [1, 8]])).then_inc(s, 16), writes=[small.res])
        P.dma("pool", "gsub", lambda e, s: e.dma_start(out=gsub.ap, in_=DAP(gsub_d, 0, [[0, 128], [1, 128]])).then_inc(s, 16), writes=[gsub.res])
        P.op("dve", lambda e: e.tensor_tensor(out=lamt[:, 0:64], in0=lamt[:, 0:64], in1=lamt[:, 64:128], op=ALU.mult), reads=[lam_res], writes=[lam_res])
        P.op("dve", lambda e: e.tensor_tensor(out=lamt[:, 128:192], in0=lamt[:, 128:192], in1=lamt[:, 192:256], op=ALU.mult), reads=[lam_res], writes=[lam_res])
        P.op("dve", lambda e: e.reduce_sum(out=small.ap[:, 1:2], in_=lamt[:, 0:64], axis=AX.X), reads=[lam_res, small.res], writes=[small.res])
        P.op("dve", lambda e: e.reduce_sum(out=small.ap[:, 2:3], in_=lamt[:, 128:192], axis=AX.X), reads=[lam_res, small.res], writes=[small.res])
        P.op("act", lambda e: e.activation(out=small.ap[:, 1:3], in_=small.ap[:, 1:3], func=AF.Exp), reads=[small.res], writes=[small.res])
        P.op("act", lambda e: e.activation(out=small.ap[:, 8:24], in_=small.ap[:, 8:24], func=AF.Exp), reads=[small.res], writes=[small.res])
        P.op("dve", lambda e: e.scalar_tensor_tensor(out=small.ap[:, 0:1], in0=small.ap[:, 2:3], scalar=-LAMBDA_INIT, in1=small.ap[:, 1:2],
                                                     op0=ALU.add, op1=ALU.subtract), reads=[small.res], writes=[small.res])
        P.op("dve", lambda e: e.tensor_scalar(out=gsub.ap, in0=gsub.ap, scalar1=1.0 - LAMBDA_INIT, scalar2=None, op0=ALU.mult),
             reads=[gsub.res], writes=[gsub.res])
        neglam = small.ap[:, 0:1]

        persist_mark = A.off

        unit_res = [Res() for _ in range(NUNIT)]

        def wunit_ap(u):
            return DAP(wscr, u * 128 * 4096, [[4096, 128], [1, 4096]])

        if "0" in phases:
            A.off = persist_mark
            gpre = Buf(f32v(48), "gpre")
            for i in range(3):
                P.dma("pool", "gpre", lambda e, s, i=i: e.dma_start(out=gpre.ap[:, i * 16:(i + 1) * 16],
                                                                    in_=DAP(g_pre_d[i], 0, [[1, 128], [128, 16]]), allow_slow_non_contiguous=True).then_inc(s, 16), writes=[gpre.res])
            NST = 3
            stg = [Buf(f32v(4096).rearrange("p (a b) -> p a b", a=16), "st%d" % i) for i in range(NST)]
            NOB = 4
            obs = [Buf(bfv(4096).rearrange("p (a b) -> p a b", a=16), "ob%d" % i) for i in range(NOB)]
            cnt = {"st": 0, "ob": 0, "eng": 0}

            def next_st():
                b = stg[cnt["st"] % NST]
                cnt["st"] += 1
                return b

            def next_ob():
                b = obs[cnt["ob"] % NOB]
                cnt["ob"] += 1
                return b

            def cast(out_ap, in_ap, gi, reads, writes, nk=16):
                engs = ("dve", "pool")
                eng = engs[cnt["eng"] % 2]
                cnt["eng"] += 1
                if gi is None:
                    P.op(eng, lambda e: e.tensor_copy(out=out_ap, in_=in_ap), reads=reads, writes=writes)
                else:
                    gv = gpre.ap[:, gi * 16:gi * 16 + nk]
                    gb = bass.AP(gv.tensor, gv.offset, [list(gv.ap[0]), [1, nk], [0, in_ap.shape[2]]])
                    P.op(eng, lambda e: e.tensor_tensor(out=out_ap, in0=in_ap, in1=gb, op=ALU.mult), reads=list(reads) + [gpre.res], writes=writes)

            def store_unit(ob, u):
                P.dma("pool", ob.lane, lambda e, s: e.dma_start(out=wunit_ap(u), in_=ob.ap.rearrange("p a b -> p (a b)")).then_inc(s, 16),
                      reads=[ob.res], writes=[unit_res[u]])

            def conv_gu(w_d, ubase, gi):
                for jp in range(22):
                    sg_, su_ = next_st(), next_st()
                    P.dma("sp", sg_.lane, lambda e, s, sg_=sg_, jp=jp: e.dma_start(
                        out=sg_.ap, in_=DAP(w_d, jp * 256, [[2 * DFF, 128], [128 * 2 * DFF, 16], [1, 256]])).then_inc(s, 16), writes=[sg_.res])
                    P.dma("sp", su_.lane, lambda e, s, su_=su_, jp=jp: e.dma_start(
                        out=su_.ap, in_=DAP(w_d, DFF + jp * 256, [[2 * DFF, 128], [128 * 2 * DFF, 16], [1, 256]])).then_inc(s, 16), writes=[su_.res])
                    for q in range(2):
                        ob = next_ob()
                        cast(ob.ap[:, :, 0:128], sg_.ap[:, :, q * 128:(q + 1) * 128], gi, [sg_.res], [ob.res])
                        cast(ob.ap[:, :, 128:256], su_.ap[:, :, q * 128:(q + 1) * 128], gi, [su_.res], [ob.res])
                        store_unit(ob, ubase + 2 * jp + q)

            def conv_down(w_d, ubase):
                for dc in range(8):
                    for fu in range(3):
                        nf = 16 if fu < 2 else 12
                        s_ = next_st()
                        P.dma("sp", s_.lane, lambda e, s, s_=s_, dc=dc, fu=fu, nf=nf: e.dma_start(
                            out=s_.ap[:, 0:nf, :], in_=DAP(w_d, fu * 16 * 128 * D + dc * 256, [[D, 128], [128 * D, nf], [1, 256]])).then_inc(s, 16),
                            writes=[s_.res])
                        ob = next_ob()
                        cast(ob.ap[:, 0:nf, :], s_.ap[:, 0:nf, :], None, [s_.res], [ob.res])
                        store_unit(ob, ubase + dc * 3 + fu)

            def conv_in():
                chunks = []
                for h in range(8):
                    chunks.append([(0, h * 128, 128)])
                for h in range(8):
                    chunks.append([(0, 1024 + h * 128, 128)])
                for j in range(8):
                    pi, g = j // 4, j % 4
                    chunks.append([(0, 3072 + ((2 * pi) * 4 + g) * 64, 64), (64, 3072 + ((2 * pi + 1) * 4 + g) * 64, 64)])
                for c in range(2):
                    chunks.append([(0, 4096 + c * 128, 128)])
                for u in range(13):
                    s_ = next_st()
                    pieces = []
                    for c2 in range(2):
                        for (d0, s0, n) in chunks[2 * u + c2]:
                            pieces.append((c2 * 128 + d0, s0, n))
                    for (d0, s0, n) in pieces:
                        P.dma("sp", s_.lane, lambda e, s, s_=s_, d0=d0, s0=s0, n=n: e.dma_start(
                            out=s_.ap[:, :, d0:d0 + n], in_=DAP(win_d, s0, [[DIN, 128], [128 * DIN, 16], [1, n]])).then_inc(s, 16), writes=[s_.res])
                    ob = next_ob()
                    cast(ob.ap, s_.ap, 1, [s_.res], [ob.res])
                    store_unit(ob, U_INF + u)
                for u in range(5):
                    s0 = 2048 + u * 256 if u < 4 else 4352
                    s_ = next_st()
                    P.dma("sp", s_.lane, lambda e, s, s_=s_, s0=s0: e.dma_start(
                        out=s_.ap, in_=DAP(win_d, s0, [[DIN, 128], [128 * DIN, 16], [1, 256]])).then_inc(s, 16), writes=[s_.res])
                    ob = next_ob()
                    cast(ob.ap, s_.ap, 1, [s_.res], [ob.res])
                    store_unit(ob, U_INT + u)

            def conv_out():
                for dc in range(8):
                    s_ = next_st()
                    P.dma("sp", s_.lane, lambda e, s, s_=s_, dc=dc: e.dma_start(
                        out=s_.ap[:, 0:8, :], in_=DAP(wout_d, dc * 256, [[D, 128], [128 * D, 8], [1, 256]])).then_inc(s, 16), writes=[s_.res])
                    for pi in range(2):
                        for half in range(2):
                            r0 = 1024 + (2 * pi + half) * 256
                            P.dma("sp", s_.lane, lambda e, s, s_=s_, dc=dc, pi=pi, half=half, r0=r0: e.dma_start(
                                out=s_.ap[half * 64:(half + 1) * 64, 8 + pi * 4:12 + pi * 4, :],
                                in_=DAP(wout_d, r0 * D + dc * 256, [[D, 64], [64 * D, 4], [1, 256]])).then_inc(s, 16), writes=[s_.res])
                    ob = next_ob()
                    cast(ob.ap, s_.ap, None, [s_.res], [ob.res])
                    store_unit(ob, U_OUT + dc)

            conv_gu(wgu_d[0], U_GU1, 0)
            conv_down(wdn_d[0], U_D1)
            conv_in()
            conv_out()
            conv_gu(wgu_d[1], U_GU2, 2)
            conv_down(wdn_d[1], U_D2)
            P.barrier()

        def setup_ffn_region():
            A.off = persist_mark
            R = {}
            R["xres"] = [Buf(f32v(D), "xres%d" % i) for i in range(2)]
            R["ffo"] = [Buf(f32v(D), None) for i in range(4)]
            R["xnb"] = [Buf(bfv(D), None) for i in range(4)]
            R["xnT"] = [Buf(bfv(16 * 512).rearrange("p (a b) -> p a b", a=16), "xnT%d" % i) for i in range(2)]
            R["actT"] = Buf(bfv(NFC * 512).rearrange("p (a b) -> p a b", a=NFC), None)
            R["wslot"] = [Buf(bfv(4096).rearrange("p (a b) -> p a b", a=16), "w%d" % i) for i in range(NSLOT)]
            R["sg"] = [Buf(f32v(512), None) for i in range(2)]
            R["stage"] = [Buf(bfv(512), "stg%d" % i) for i in range(4)]
            R["junk"] = Buf(bfv(256), None)
            R["ssd"] = Buf(f32v(64), None)
            return R

        class WStream:
            def __init__(self, R, seq):
                self.slots = R["wslot"]
                self.seq = seq
                self.nload = 0
                self.nuse = 0

            def _load(self):
                n = self.nload
                u = self.seq[n]
                sl = self.slots[n % NSLOT]
                P.dma("sp", sl.lane, lambda e, s, sl=sl, u=u: e.dma_start(out=sl.ap.rearrange("p a b -> p (a b)"), in_=wunit_ap(u)).then_inc(s, 16),
                      reads=[unit_res[u]], writes=[sl.res])
                self.nload += 1

            def next(self):
                while self.nload < len(self.seq) and self.nload < self.nuse + NSLOT:
                    self._load()
                sl = self.slots[self.nuse % NSLOT]
                self.nuse += 1
                return sl

        ev_cnt = {"n": 0}

        def rstd_sqrt(col_ap, res, n):
            P.op("dve", lambda e: e.tensor_scalar(out=col_ap, in0=col_ap, scalar1=1.0 / n, scalar2=EPS, op0=ALU.mult, op1=ALU.add), reads=[res], writes=[res])
            P.op("act", lambda e: e.activation(out=col_ap, in_=col_ap, func=AF.Sqrt), reads=[res], writes=[res])
            P.op("dve", lambda e: e.reciprocal(out=col_ap, in_=col_ap), reads=[res], writes=[res])

        def norm_nonpe(R, src, ts, sidx):
            col = stats[:, sidx:sidx + 1]
            sres = stats_res[sidx % 16]
            xnb = R["xnb"][ts]
            P.op("act", lambda e: e.activation(out=xnb.ap, in_=src.ap, func=AF.Square, accum_out=col), reads=[src.res], writes=[xnb.res, sres])
            rstd_sqrt(col, sres, D)
            P.op("dve", lambda e: e.tensor_scalar(out=xnb.ap, in0=src.ap, scalar1=col, scalar2=None, op0=ALU.mult), reads=[src.res, sres], writes=[xnb.res])

        def norm_pe(R, ts, xnT):
            xnb = R["xnb"][ts]
            for hb in range(2):
                b = 4 + hb * 2
                pb = bank_bf(b)
                for c in range(8):
                    kc = hb * 8 + c
                    P.op("pe", lambda e, pb=pb, c=c, kc=kc: e.transpose(pb[:, c * 128:(c + 1) * 128], xnb.ap[:, kc * 128:(kc + 1) * 128], ident.ap),
                         reads=[xnb.res, ident.res], writes=[bank_res[b]])
                dst = xnT.ap[:, hb * 8:(hb + 1) * 8, ts * 128:(ts + 1) * 128]
                srcp = pb.rearrange("p (a b) -> p a b", a=8)
                if hb == 0:
                    P.op("dve", lambda e, dst=dst, srcp=srcp: e.tensor_copy(out=dst, in_=srcp), reads=[bank_res[b]], writes=[xnT.res])
                else:
                    P.op("act", lambda e, dst=dst, srcp=srcp: e.activation(out=dst, in_=srcp, func=AF.Copy), reads=[bank_res[b]], writes=[xnT.res])

        def acc_view(setidx, ts):
            b = 4 + 2 * setidx + ts // 2
            return b, psb[:, b, (ts % 2) * 256:(ts % 2) * 256 + 256]

        def tokmajor_stage(R, ws, lhs_buf, nchunks, ndc, evac):
            nfu = (nchunks + 15) // 16
            for dc in range(ndc):
                setidx = dc % 2
                for fu in range(nfu):
                    nf = min(16, nchunks - fu * 16)
                    sl = ws.next()
                    for ts in range(4):
                        b, acc = acc_view(setidx, ts)
                        for fl in range(nf):
                            f = fu * 16 + fl
                            first = (fu == 0 and fl == 0 and ts % 2 == 0)
                            last = (f == nchunks - 1)
                            P.op("pe", lambda e, acc=acc, f=f, fl=fl, ts=ts, sl=sl, first=first, last=last: e.matmul(
                                acc, lhs_buf.ap[:, f, ts * 128:(ts + 1) * 128], sl.ap[:, fl, :], start=first, stop=last, skip_group_check=True),
                                reads=[lhs_buf.res, sl.res], writes=[bank_res[b]])
                for ts in range(4):
                    b, acc = acc_view(setidx, ts)
                    evac(dc, ts, acc, b)

        def evac_to_ffo(R):
            ssd = R["ssd"]

            def evac(dc, ts, acc, b):
                ffo = R["ffo"][ts]
                fsl = ffo.ap[:, dc * 256:(dc + 1) * 256]
                if ts < 2:
                    P.op("dve", lambda e: e.tensor_copy(out=fsl, in_=acc), reads=[bank_res[b]], writes=[ffo.res])
                else:
                    P.op("act", lambda e: e.activation(out=fsl, in_=acc, func=AF.Copy), reads=[bank_res[b]], writes=[ffo.res])
                jk = R["junk"]
                P.op("act", lambda e: e.activation(out=jk.ap, in_=fsl, func=AF.Square, accum_out=ssd.ap[:, ts * 8 + dc:ts * 8 + dc + 1]),
                     reads=[ffo.res], writes=[jk.res, ssd.res])
            return evac

        def residual_epilogue(R, ts, src_ap_dram, src_res, gpost, dst_ap_dram, dst_res, sidx, after=None):
            ssd = R["ssd"]
            col = stats[:, sidx:sidx + 1]
            sres = stats_res[sidx % 16]
            P.op("dve", lambda e: e.reduce_sum(out=col, in_=ssd.ap[:, ts * 8:(ts + 1) * 8], axis=AX.X), reads=[ssd.res], writes=[sres])
            rstd_sqrt(col, sres, D)
            xb = R["xres"][ts % 2]
            P.dma("pool", xb.lane, lambda e, s: e.dma_start(out=xb.ap, in_=src_ap_dram).then_inc(s, 16), reads=[src_res], writes=[xb.res])
            ffo = R["ffo"][ts]
            P.op("dve", lambda e: e.scalar_tensor_tensor(out=ffo.ap, in0=ffo.ap, scalar=col, in1=gpost.ap, op0=ALU.mult, op1=ALU.mult),
                 reads=[ffo.res, sres, gpost.res], writes=[ffo.res])
            P.op("pool", lambda e: e.tensor_tensor(out=xb.ap, in0=xb.ap, in1=ffo.ap, op=ALU.add), reads=[xb.res, ffo.res], writes=[xb.res])
            P.dma("pool", xb.lane, lambda e, s: e.dma_start(out=dst_ap_dram, in_=xb.ap).then_inc(s, 16), reads=[xb.res], writes=[dst_res])
            if after is not None:
                after(ts, xb)

        def ffn_stage1(R, ws, xnT, j0=0, j1=NFC):
            actT = R["actT"]
            for j in range(j0, j1):
                sl = ws.next()
                setidx = j % 2
                bg, bu = 2 * setidx, 2 * setidx + 1
                for kc in range(NKC):
                    P.op("pe", lambda e, kc=kc, sl=sl, bg=bg: e.matmul(bank(bg), sl.ap[:, kc, 0:128], xnT.ap[:, kc, :], start=(kc == 0), stop=(kc == NKC - 1)),
                         reads=[sl.res, xnT.res], writes=[bank_res[bg]])
                for kc in range(NKC):
                    P.op("pe", lambda e, kc=kc, sl=sl, bu=bu: e.matmul(bank(bu), sl.ap[:, kc, 128:256], xnT.ap[:, kc, :], start=(kc == 0), stop=(kc == NKC - 1)),
                         reads=[sl.res, xnT.res], writes=[bank_res[bu]])
                sg = R["sg"][setidx]
                P.op("act", lambda e, sg=sg, bg=bg: e.activation(out=sg.ap, in_=bank(bg), func=AF.Silu), reads=[bank_res[bg]], writes=[sg.res])
                P.op("dve", lambda e, sg=sg, bu=bu, j=j: e.tensor_tensor(out=actT.ap[:, j, :], in0=sg.ap, in1=bank(bu), op=ALU.mult),
                     reads=[sg.res, bank_res[bu]], writes=[actT.res])

        def load_gpost(buf, gi, half):
            P.dma("pool", buf.lane, lambda e, s: e.dma_start(out=buf.ap, in_=DAP(g_post_d[gi], 0, [[0, 128], [1, D]])).then_inc(s, 16), writes=[buf.res])
            if half:
                P.op("dve", lambda e: e.tensor_scalar(out=buf.ap, in0=buf.ap, scalar1=0.5, scalar2=None, op0=ALU.mult), reads=[buf.res], writes=[buf.res])

        def rows(t, r0, n=128):
            return DAP(t, r0 * D, [[D, n], [1, D]])

        hres = [Res() for _ in range(NBLK)]
        xin_res = Res()
        yres = Res()

        Q1, Q2, Q3 = 10, 14, 36

        def ffn_pass(R, ws, ntile, src_t, src_res_fn, dst_t, dst_res_fn, gpost, hook_nonpe=None, hook_pe=None):
            sctr = [0]

            def nsid():
                sctr[0] += 1
                return sctr[0] % 16

            def pro_nonpe(t):
                for ts in range(4):
                    xb = R["xres"][ts % 2]
                    r0 = t * TT + ts * 128
                    P.dma("pool", xb.lane, lambda e, s, xb=xb, r0=r0: e.dma_start(out=xb.ap, in_=rows(src_t, r0)).then_inc(s, 16),
                          reads=[src_res_fn(r0)], writes=[xb.res])
                    norm_nonpe(R, xb, ts, nsid())

            def pro_pe(t):
                for ts in range(4):
                    norm_pe(R, ts, R["xnT"][t % 2])

            pro_nonpe(0)
            pro_pe(0)
            for t in range(ntile):
                xnT = R["xnT"][t % 2]
                ffn_stage1(R, ws, xnT, 0, Q1)
                if t > 0 and hook_pe is not None:
                    hook_pe(t - 1)
                ffn_stage1(R, ws, xnT, Q1, Q2)
                if t + 1 < ntile:
                    pro_nonpe(t + 1)
                ffn_stage1(R, ws, xnT, Q2, Q3)
                if t + 1 < ntile:
                    pro_pe(t + 1)
                ffn_stage1(R, ws, xnT, Q3, NFC)
                tokmajor_stage(R, ws, R["actT"], NFC, 8, evac_to_ffo(R))
                for ts in range(4):
                    r0 = t * TT + ts * 128
                    after = None
                    if hook_nonpe is not None:
                        after = (lambda ts_, hb, t=t: hook_nonpe(t, ts_, hb, nsid()))
                    residual_epilogue(R, ts, rows(src_t, r0), src_res_fn(r0), gpost, rows(dst_t, r0), dst_res_fn(r0), nsid(), after=after)
            if hook_pe is not None:
                hook_pe(ntile - 1)

        def ffn_unit_seq(ntile, ugu, ud, extra):
            seq = []
            for t in range(ntile):
                seq += [ugu + j for j in range(Q1)]
                if t > 0:
                    seq += extra
                seq += [ugu + j for j in range(Q1, NFC)]
                seq += [ud + j for j in range(24)]
            seq += extra
            return seq

        NTA = NT if ntiles is None else ntiles

        if "A" in phases:
            R = setup_ffn_region()
            extraA = [U_INF + j for j in range(13)] + [U_INT + j for j in range(5)]
            ws = WStream(R, ffn_unit_seq(NTA, U_GU1, U_D1, extraA))
            load_gpost(gpostA, 0, True)

            def hook_nonpe_A(t, ts, hb, sid):
                norm_nonpe(R, hb, ts, sid)

            def hook_pe_A(t):
                tok0 = t * TT
                xnT = R["xnT"][t % 2]
                for ts in range(4):
                    norm_pe(R, ts, xnT)
                for u in range(13):
                    sl = ws.next()
                    for c2 in range(2):
                        ci = 2 * u + c2
                        b = ci % 4
                        for kc in range(NKC):
                            P.op("pe", lambda e, kc=kc, sl=sl, b=b, c2=c2: e.matmul(bank(b), sl.ap[:, kc, c2 * 128:(c2 + 1) * 128], xnT.ap[:, kc, :],
                                                                                   start=(kc == 0), stop=(kc == NKC - 1)),
                                 reads=[sl.res, xnT.res], writes=[bank_res[b]])
                        sb_ = R["stage"][ci % 4]
                        if ci < 8:
                            scale, dst = 0.125, DAP(qda, ci * 128 * TOK + tok0, [[TOK, 128], [1, TT]])
                        elif ci < 16:
                            scale, dst = 1.0, DAP(kda, (ci - 8) * 128 * TOK + tok0, [[TOK, 128], [1, TT]])
                        elif ci < 24:
                            j = ci - 16
                            pi, g = j // 4, j % 4
                            scale = 0.125
                            dst = DAP(qsw, pi * 128 * NBLK * 512 + (tok0 // 128) * 512 + g * 128, [[NBLK * 512, 128], [512, 4], [1, 128]])
                        else:
                            scale, dst = 1.0, DAP(ksw, (ci - 24) * 128 * TOK + tok0, [[TOK, 128], [1, TT]])
                        if ci % 2 == 0:
                            P.op("act", lambda e, sb_=sb_, b=b, scale=scale: e.activation(out=sb_.ap, in_=bank(b), func=AF.Copy, scale=scale),
                                 reads=[bank_res[b]], writes=[sb_.res])
                        else:
                            P.op("dve", lambda e, sb_=sb_, b=b, scale=scale: e.tensor_scalar(out=sb_.ap, in0=bank(b), scalar1=scale, scalar2=None, op0=ALU.mult),
                                 reads=[bank_res[b]], writes=[sb_.res])
                        if 16 <= ci < 24:
                            src = sb_.ap.rearrange("p (a b) -> p a b", a=4)
                        else:
                            src = sb_.ap
                        P.dma("pool", sb_.lane, lambda e, s, dst=dst, src=src: e.dma_start(out=dst, in_=src).then_inc(s, 16), reads=[sb_.res])
                for u in range(5):
                    sl = ws.next()
                    setidx = u % 2
                    for ts in range(4):
                        b, acc = acc_view(setidx, ts)
                        for kc in range(NKC):
                            first = (kc == 0 and ts % 2 == 0)
                            P.op("pe", lambda e, acc=acc, kc=kc, ts=ts, sl=sl, first=first: e.matmul(
                                acc, xnT.ap[:, kc, ts * 128:(ts + 1) * 128], sl.ap[:, kc, :], start=first, stop=(kc == NKC - 1), skip_group_check=True),
                                reads=[xnT.res, sl.res], writes=[bank_res[b]])
                    for ts in range(4):
                        b, acc = acc_view(setidx, ts)
                        sb_ = R["stage"][ts]
                        r0 = tok0 + ts * 128
                        vv = sb_.ap[:, 0:256]
                        if ts < 2:
                            P.op("act", lambda e, vv=vv, acc=acc: e.activation(out=vv, in_=acc, func=AF.Copy), reads=[bank_res[b]], writes=[sb_.res])
                        else:
                            P.op("dve", lambda e, vv=vv, acc=acc: e.tensor_copy(out=vv, in_=acc), reads=[bank_res[b]], writes=[sb_.res])
                        if u < 4:
                            dst = DAP(vda, (2 * u) * TOK * 128 + r0 * 128, [[128, 128], [TOK * 128, 2], [1, 128]])
                            src = vv.rearrange("p (a b) -> p a b", a=2)
                        else:
                            dst = DAP(vsw, r0 * 64, [[64, 128], [TOK * 64, 4], [1, 64]])
                            src = vv.rearrange("p (a b) -> p a b", a=4)
                        P.dma("pool", sb_.lane, lambda e, s, dst=dst, src=src: e.dma_start(out=dst, in_=src).then_inc(s, 16), reads=[sb_.res])

            ffn_pass(R, ws, NTA, x_d, lambda r0: xin_res, hscr, lambda r0: hres[r0 // 128], gpostA,
                     hook_nonpe=hook_nonpe_A, hook_pe=hook_pe_A)
            P.barrier()

        if "B" in phases:
            A.off = persist_mark
            rbx_f = Buf(f32v(24), "rbx")
            rbx = Buf(bfv(24), None)
            P.op("dve", lambda e: e.memset(rbx_f.ap[0:64, :], 1.0), writes=[rbx_f.res])
            P.dma("pool", "rbx", lambda e, s: e.dma_start(out=rbx_f.ap[0:32, :], in_=DAP(relb_d, 0, [[24, 32], [1, 24]])).then_inc(s, 16), writes=[rbx_f.res])
            P.op("dve", lambda e: e.tensor_copy(out=rbx.ap[0:64, :], in_=rbx_f.ap[0:64, :]), reads=[rbx_f.res], writes=[rbx.res])
            oht = Buf(bfv(LDA + LSW), "oht")
            P.dma("pool", "oht", lambda e, s: e.dma_start(out=oht.ap[0:33, 0:LDA], in_=DAP(ohda_d, 0, [[LDA, 33], [1, LDA]])).then_inc(s, 16), writes=[oht.res])
            P.dma("pool", "oht", lambda e, s: e.dma_start(out=oht.ap[0:33, LDA:LDA + LSW], in_=DAP(ohsw_d, 0, [[LSW, 33], [1, LSW]])).then_inc(s, 16), writes=[oht.res])
            tabs = Buf(bfv(LDA + LSW), "tabs")
            col0 = 0
            for (c0, n) in ((0, 512), (512, 512), (1024, 256), (LDA, 512)):
                b = (c0 // 512) % 4
                P.op("pe", lambda e, c0=c0, n=n, b=b: e.matmul(psb[0:24, b, 0:n], rbx.ap[0:33, :], oht.ap[0:33, c0:c0 + n], start=True, stop=True),
                     reads=[rbx.res, oht.res], writes=[bank_res[b]])
                P.op("dve", lambda e, c0=c0, n=n, b=b: e.tensor_copy(out=tabs.ap[0:24, c0:c0 + n], in_=psb[0:24, b, 0:n]), reads=[bank_res[b]], writes=[tabs.res])
            tab_res = Res()
            P.dma("pool", "tabs", lambda e, s: e.dma_start(out=DAP(tabscr, 0, [[2048, 24], [1, LDA + LSW]]), in_=tabs.ap[0:24, :]).then_inc(s, 16),
                  reads=[tabs.res], writes=[tab_res])
            strip_da = Buf(bfv(8 * 1152).rearrange("p (a b) -> p a b", a=8), None)
            strip_sw = Buf(bfv(16 * 384).rearrange("p (a b) -> p a b", a=16), None)
            hank = [Buf(bfv(1152), "hank%d" % i) for i in range(2)]
            for h in range(8):
                hk = hank[h % 2]
                P.dma("pool", hk.lane, lambda e, s, hk=hk, h=h: e.dma_start(out=hk.ap, in_=DAP(tabscr, h * 2048, [[1, 128], [1, 1152]])).then_inc(s, 16),
                      reads=[tab_res], writes=[hk.res])
                rv = hk.ap
                rev = bass.AP(rv.tensor, rv.offset + 1151, [list(rv.ap[0]), [-1, 1152]])
                P.op("dve", lambda e, h=h, rev=rev: e.tensor_copy(out=strip_da.ap[:, h, :], in_=rev), reads=[hk.res], writes=[strip_da.res])
            for hq in range(16):
                hk = hank[hq % 2]
                P.dma("pool", hk.lane, lambda e, s, hk=hk, hq=hq: e.dma_start(out=hk.ap[:, 0:384], in_=DAP(tabscr, (8 + hq) * 2048 + LDA, [[1, 128], [1, 384]])).then_inc(s, 16),
                      reads=[tab_res], writes=[hk.res])
                rv = hk.ap
                rev = bass.AP(rv.tensor, rv.offset + 383, [list(rv.ap[0]), [-1, 384]])
                P.op("dve", lambda e, hq=hq, rev=rev: e.tensor_copy(out=strip_sw.ap[:, hq, :], in_=rev), reads=[hk.res], writes=[strip_sw.res])

            SMAX = max(seqs)
            NBMAX = SMAX // 128
            VP = 130
            qT = [Buf(bfv(SMAX), "qT%d" % i) for i in range(2)]
            kT = [Buf(bfv(SMAX), "kT%d" % i) for i in range(2)]
            vv_ = [Buf(bfv(NBMAX * VP).rearrange("p (a b) -> p a b", a=NBMAX), "v%d" % i) for i in range(2)]
            for i in range(2):
                P.op("pool", lambda e, i=i: e.memset(vv_[i].ap[:, :, 128:130], 1.0), writes=[vv_[i].res])
            eT = [Buf(bfv(1024), None) for i in range(3)]
            accs = Buf(f32v(8 * 129), None)
            o1 = Buf(f32v(512), None)
            o2 = Buf(f32v(512), None)
            ob16 = [Buf(bfv(512), None) for i in range(2)]
            junkB = Buf(bfv(128), None)
            mstage = [Buf(bfv(512), "mst%d" % i) for i in range(2)]
            rz = f32v(16)
            rz_res = Res()
            it = {"qk": 0, "e": 0, "ms": 0, "ob": 0}

            def da_head(si, h, hidx):
                S = seqs[si]
                t0 = seq_off[si]
                nb = S // 128
                q_, k_, v_ = qT[hidx % 2], kT[hidx % 2], vv_[hidx % 2]
                P.dma("sp", q_.lane, lambda e, s: e.dma_start(out=q_.ap[:, 0:S], in_=DAP(qda, h * 128 * TOK + t0, [[TOK, 128], [1, S]])).then_inc(s, 16), writes=[q_.res])
                P.dma("sp", k_.lane, lambda e, s: e.dma_start(out=k_.ap[:, 0:S], in_=DAP(kda, h * 128 * TOK + t0, [[TOK, 128], [1, S]])).then_inc(s, 16), writes=[k_.res])
                P.dma("sp", v_.lane, lambda e, s: e.dma_start(out=v_.ap[:, 0:nb, 0:128], in_=DAP(vda, h * TOK * 128 + t0 * 128, [[128, 128], [128 * 128, nb], [1, 128]])).then_inc(s, 16),
                      writes=[v_.res])
                clo = small.ap[:, 32 + h:33 + h]
                chi = small.ap[:, 40 + h:41 + h]
                units = []
                for qc in range(S // 512):
                    for kb in range(nb):
                        units.append((qc, kb))

                def front(qc, kb):
                    Dd = kb - 4 * qc
                    near = (-1 <= Dd <= 4)
                    setidx = it["qk"] % 2
                    it["qk"] += 1
                    bA, bB = 2 * setidx, 2 * setidx + 1
                    for m, b in ((0, bA), (1, bB)):
                        P.op("pe", lambda e, m=m, b=b: e.matmul(
                            bank(b), k_.ap[m * 64:(m + 1) * 64, kb * 128:(kb + 1) * 128], q_.ap[m * 64:(m + 1) * 64, qc * 512:(qc + 1) * 512],
                            start=True, stop=(not near)), reads=[k_.res, q_.res], writes=[bank_res[b]])
                    if near:
                        m0 = 4 - Dd
                        for b in (bA, bB):
                            P.op("pe", lambda e, b=b: e.matmul(bank(b), ident.ap, strip_da.ap[:, h, m0 * 128:m0 * 128 + 512], start=False, stop=True),
                                 reads=[ident.res, strip_da.res], writes=[bank_res[b]])
                    et = eT[it["e"] % 3]
                    it["e"] += 1
                    pair = psb[:, bA:bA + 2, :].rearrange("p a b -> p (a b)")
                    if near:
                        P.op("act", lambda e: e.activation(out=et.ap, in_=pair, func=AF.Exp),
                             reads=[bank_res[bA], bank_res[bB]], writes=[et.res])
                    else:
                        cb = clo if Dd < 0 else chi
                        P.op("act", lambda e: e.activation(out=et.ap, in_=pair, func=AF.Exp, bias=cb),
                             reads=[bank_res[bA], bank_res[bB], small.res], writes=[et.res])
                    return et

                def back(qc, kb, et):
                    for m in range(2):
                        for qs in range(4):
                            a = m * 4 + qs
                            b = 4 + a // 3
                            c0 = (a % 3) * 129
                            first = (kb == 0 and a % 3 == 0)
                            P.op("pe", lambda e, m=m, qs=qs, b=b, c0=c0, first=first: e.matmul(
                                psb[:, b, c0:c0 + 129], et.ap[:, m * 512 + qs * 128:m * 512 + (qs + 1) * 128], v_.ap[:, kb, 0:129],
                                start=first, stop=(kb == nb - 1), skip_group_check=True), reads=[et.res, v_.res], writes=[bank_res[b]])
                    if kb == nb - 1:
                        epilogue(qc)

                def epilogue(qc):
                    for bi in range(3):
                        n = 387 if bi < 2 else 258
                        if bi != 1:
                            P.op("dve", lambda e, bi=bi, n=n: e.tensor_copy(out=accs.ap[:, bi * 387:bi * 387 + n], in_=psb[:, 4 + bi, 0:n]),
                                 reads=[bank_res[4 + bi]], writes=[accs.res])
                        else:
                            P.op("act", lambda e, bi=bi, n=n: e.activation(out=accs.ap[:, bi * 387:bi * 387 + n], in_=psb[:, 4 + bi, 0:n], func=AF.Copy),
                                 reads=[bank_res[4 + bi]], writes=[accs.res])
                    acc8 = accs.ap[:, 0:1032].rearrange("p (a c) -> p a c", c=129)

                    def bc(v, n):
                        return bass.AP(v.tensor, v.offset, [list(v.ap[0]), [1, v.shape[1]], [0, n]])
                    o1v = o1.ap.rearrange("p (a c) -> p a c", a=4)
                    o2v = o2.ap.rearrange("p (a c) -> p a c", a=4)
                    P.op("dve", lambda e: e.reciprocal(out=rz[:, 0:8], in_=acc8[:, :, 128]), reads=[accs.res], writes=[rz_res])
                    P.op("dve", lambda e: e.tensor_scalar(out=rz[:, 4:8], in0=rz[:, 4:8], scalar1=neglam, scalar2=None, op0=ALU.mult),
                         reads=[rz_res, small.res], writes=[rz_res])
                    P.op("dve", lambda e: e.tensor_tensor(out=o1v, in0=acc8[:, 0:4, 0:128], in1=bc(rz[:, 0:4], 128), op=ALU.mult),
                         reads=[accs.res, rz_res], writes=[o1.res])
                    P.op("dve", lambda e: e.tensor_tensor(out=o2v, in0=acc8[:, 4:8, 0:128], in1=bc(rz[:, 4:8], 128), op=ALU.mult),
                         reads=[accs.res, rz_res], writes=[o2.res])
                    P.op("dve", lambda e: e.tensor_tensor(out=o1.ap, in0=o1.ap, in1=o2.ap, op=ALU.add), reads=[o1.res, o2.res], writes=[o1.res])
                    P.op("dve", lambda e: e.tensor_tensor(out=o2.ap, in0=o1.ap, in1=o1.ap, op=ALU.mult), reads=[o1.res], writes=[o2.res])
                    P.op("dve", lambda e: e.reduce_sum(out=rz[:, 8:12], in_=o2v, axis=AX.X), reads=[o2.res, rz_res], writes=[rz_res])
                    P.op("dve", lambda e: e.tensor_scalar(out=rz[:, 8:12], in0=rz[:, 8:12], scalar1=1.0 / 128, scalar2=EPS, op0=ALU.mult, op1=ALU.add),
                         reads=[rz_res], writes=[rz_res])
                    P.op("act", lambda e: e.activation(out=rz[:, 8:12], in_=rz[:, 8:12], func=AF.Ln), reads=[rz_res], writes=[rz_res])
                    P.op("act", lambda e: e.activation(out=rz[:, 8:12], in_=rz[:, 8:12], func=AF.Exp, scale=-0.5), reads=[rz_res], writes=[rz_res])
                    P.op("dve", lambda e: e.tensor_tensor(out=o1v, in0=o1v, in1=bc(rz[:, 8:12], 128), op=ALU.mult), reads=[o1.res, rz_res], writes=[o1.res])
                    ob = ob16[it["ob"] % 2]
                    it["ob"] += 1
                    gv = gsub.ap
                    gbc = bass.AP(gv.tensor, gv.offset, [list(gv.ap[0]), [0, 4], [1, 128]])
                    P.op("dve", lambda e: e.tensor_tensor(out=ob.ap.rearrange("p (a c) -> p a c", a=4), in0=o1v, in1=gbc, op=ALU.mult),
                         reads=[o1.res, gsub.res], writes=[ob.res])

                    def fin():
                        ms = mstage[it["ms"] % 2]
                        it["ms"] += 1
                        for qs in range(4):
                            P.op("pe", lambda e, qs=qs: e.transpose(bank_bf(7)[:, qs * 128:(qs + 1) * 128], ob.ap[:, qs * 128:(qs + 1) * 128], ident.ap),
                                 reads=[ob.res, ident.res], writes=[bank_res[7]])
                        P.op("dve", lambda e: e.tensor_copy(out=ms.ap, in_=bank_bf(7)[:, 0:512]), reads=[bank_res[7]], writes=[ms.res])
                        P.dma("pool", ms.lane, lambda e, s: e.dma_start(out=DAP(mixT, h * 128 * TOK + t0 + qc * 512, [[TOK, 128], [1, 512]]), in_=ms.ap).then_inc(s, 16),
                              reads=[ms.res])
                    deferred.append([8, fin])

                deferred = []

                def tick():
                    for d in deferred:
                        d[0] -= 1
                    while deferred and deferred[0][0] <= 0:
                        deferred.pop(0)[1]()

                prev = None
                for (qc, kb) in units:
                    et = front(qc, kb)
                    if prev is not None:
                        back(*prev)
                    tick()
                    prev = (qc, kb, et)
                back(*prev)
                while deferred:
                    deferred.pop(0)[1]()

            SEGB = 8
            qs_t = [Buf(bfv(SEGB * 512).rearrange("p (a g q) -> p a g q", a=SEGB, g=4), "qs%d" % i) for i in range(2)]
            ks_t = [Buf(bfv((SEGB + 2) * 128), "ks%d" % i) for i in range(2)]
            VS = 66
            vs_t = [Buf(bfv((SEGB + 2) * 2 * VS).rearrange("p (a h d) -> p a h d", a=SEGB + 2, h=2), "vs%d" % i) for i in range(2)]
            for i in range(2):
                P.op("pool", lambda e, i=i: e.memset(vs_t[i].ap[:, :, :, 64:66], 1.0), writes=[vs_t[i].res])
            eS = [Buf(bfv(512), None) for i in range(3)]
            accS = Buf(f32v(4 * VS), None)
            zz = f32v(8)
            zz_res = Res()
            osw = [Buf(bfv(512).rearrange("p (g d) -> p g d", g=4), None) for i in range(2)]
            sit = {"b": 0, "e": 0, "acc": 0, "o": 0, "ms": 0}

            def sw_seg(si, pi, seg, sidx):
                S = seqs[si]
                t0 = seq_off[si]
                nb = S // 128
                i0 = seg * SEGB
                i1 = min(nb, i0 + SEGB)
                kb0 = max(0, i0 - 1)
                kb1 = min(nb, i1 + 1)
                nkb = kb1 - kb0
                nq = i1 - i0
                q_, k_, v_ = qs_t[sidx % 2], ks_t[sidx % 2], vs_t[sidx % 2]
                blk0 = t0 // 128 + i0
                P.dma("sp", q_.lane, lambda e, s: e.dma_start(out=q_.ap[:, 0:nq].rearrange("p a g q -> p (a g q)"),
                                                              in_=DAP(qsw, pi * 128 * NBLK * 512 + blk0 * 512, [[NBLK * 512, 128], [1, nq * 512]])).then_inc(s, 16), writes=[q_.res])
                P.dma("sp", k_.lane, lambda e, s: e.dma_start(out=k_.ap[:, 0:nkb * 128], in_=DAP(ksw, pi * 128 * TOK + t0 + kb0 * 128, [[TOK, 128], [1, nkb * 128]])).then_inc(s, 16),
                      writes=[k_.res])
                for half in range(2):
                    kvh = 2 * pi + half
                    P.dma("sp", v_.lane, lambda e, s, half=half, kvh=kvh: e.dma_start(
                        out=v_.ap[:, 0:nkb, half, 0:64], in_=DAP(vsw, kvh * TOK * 64 + (t0 + kb0 * 128) * 64, [[64, 128], [128 * 64, nkb], [1, 64]])).then_inc(s, 16),
                        writes=[v_.res])
                pend = []

                def flush(force=False):
                    for d in pend:
                        d[0] -= 1
                    ready = [d for d in pend if force or d[0] <= 0]
                    pend[:] = [d for d in pend if not (force or d[0] <= 0)]
                    for d in ready:
                        d[1]()

                for i in range(i0, i1):
                    ow = osw[sit["o"] % 2]
                    sit["o"] += 1
                    for half in range(2):
                        hq0 = (2 * pi + half) * 4
                        kbs = [kb for kb in (i - 1, i, i + 1) if 0 <= kb < nb]
                        ab = 4 + sit["acc"] % 2
                        sit["acc"] += 1
                        for idx, kb in enumerate(kbs):
                            rr = i - kb + 1
                            b = sit["b"] % 4
                            sit["b"] += 1
                            P.op("pe", lambda e, b=b, kb=kb, i=i, half=half: e.matmul(
                                bank(b), k_.ap[half * 64:(half + 1) * 64, (kb - kb0) * 128:(kb - kb0 + 1) * 128], q_.ap[half * 64:(half + 1) * 64, i - i0],
                                start=True, stop=False), reads=[k_.res, q_.res], writes=[bank_res[b]])
                            P.op("pe", lambda e, b=b, rr=rr, hq0=hq0: e.matmul(bank(b), ident.ap, strip_sw.ap[:, hq0:hq0 + 4, rr * 128:(rr + 1) * 128], start=False, stop=True),
                                 reads=[ident.res, strip_sw.res], writes=[bank_res[b]])
                            et = eS[sit["e"] % 3]
                            sit["e"] += 1
                            P.op("act", lambda e, et=et, b=b: e.activation(out=et.ap, in_=bank(b), func=AF.Exp), reads=[bank_res[b]], writes=[et.res])
                            flush()

                            def pv(et=et, ab=ab, kb=kb, half=half, idx=idx, last=(idx == len(kbs) - 1)):
                                for g in range(4):
                                    P.op("pe", lambda e, g=g: e.matmul(
                                        psb[:, ab, g * VS:g * VS + 65], et.ap[:, g * 128:(g + 1) * 128], v_.ap[:, kb - kb0, half, 0:65],
                                        start=(idx == 0 and g == 0), stop=last, skip_group_check=True), reads=[et.res, v_.res], writes=[bank_res[ab]])
                            pend.append([1, pv])

                        def epi(ab=ab, hq0=hq0, half=half, ow=ow):
                            P.op("dve", lambda e: e.tensor_copy(out=accS.ap, in_=psb[:, ab, 0:4 * VS]), reads=[bank_res[ab]], writes=[accS.res])
                            a3 = accS.ap.rearrange("p (g d) -> p g d", g=4)
                            P.op("dve", lambda e: e.tensor_tensor(out=zz[:, 0:4], in0=a3[:, :, 64], in1=small.ap[:, 8 + hq0:12 + hq0], op=ALU.add),
                                 reads=[accS.res, small.res], writes=[zz_res])
                            P.op("dve", lambda e: e.reciprocal(out=zz[:, 0:4], in_=zz[:, 0:4]), reads=[zz_res], writes=[zz_res])
                            zv = zz[:, 0:4]
                            zb = bass.AP(zv.tensor, zv.offset, [list(zv.ap[0]), [1, 4], [0, 64]])
                            P.op("dve", lambda e: e.tensor_tensor(out=ow.ap[:, :, half * 64:(half + 1) * 64], in0=a3[:, :, 0:64], in1=zb, op=ALU.mult),
                                 reads=[accS.res, zz_res], writes=[ow.res])
                        pend.append([1, epi])

                    def fin(ow=ow, i=i):
                        for g in range(4):
                            P.op("pe", lambda e, g=g: e.transpose(bank_bf(7)[:, g * 128:(g + 1) * 128], ow.ap[:, g, :], ident.ap),
                                 reads=[ow.res, ident.res], writes=[bank_res[7]])
                        ms = mstage[sit["ms"] % 2]
                        sit["ms"] += 1
                        P.op("dve", lambda e: e.tensor_copy(out=ms.ap, in_=bank_bf(7)[:, 0:512]), reads=[bank_res[7]], writes=[ms.res])
                        P.dma("pool", ms.lane, lambda e, s: e.dma_start(
                            out=DAP(mixT, (8 + pi * 4) * 128 * TOK + t0 + i * 128, [[TOK, 128], [128 * TOK, 4], [1, 128]]),
                            in_=ms.ap.rearrange("p (g q) -> p g q", g=4)).then_inc(s, 16), reads=[ms.res])
                    pend.append([3, fin])
                flush(True)

            hidx = 0
            sidx = 0
            for si in range(len(seqs)):
                for h in range(8):
                    da_head(si, h, hidx)
                    hidx += 1
                nseg = (seqs[si] // 128 + SEGB - 1) // SEGB
                for pi in range(2):
                    for seg in range(nseg):
                        sw_seg(si, pi, seg, sidx)
                        sidx += 1
            P.barrier()

        if "C" in phases:
            R = setup_ffn_region()
            seqC = []
            for t in range(NT):
                seqC += [U_OUT + j for j in range(8)]
            seqC += ffn_unit_seq(NT, U_GU2, U_D2, [])
            ws = WStream(R, seqC)
            load_gpost(gpostA, 1, False)
            sctr = [0]

            def next_sidx2():
                sctr[0] += 1
                return sctr[0] % 16

            def load_mix(t):
                xnT = R["xnT"][t % 2]
                tok0 = t * TT
                P.dma("pool", xnT.lane, lambda e, s: e.dma_start(out=xnT.ap, in_=DAP(mixT, tok0, [[TOK, 128], [128 * TOK, 16], [1, TT]])).then_inc(s, 16),
                      writes=[xnT.res])

            load_mix(0)
            for t in range(NT):
                tok0 = t * TT
                if t + 1 < NT:
                    load_mix(t + 1)
                tokmajor_stage(R, ws, R["xnT"][t % 2], 16, 8, evac_to_ffo(R))
                for ts in range(4):
                    r0 = tok0 + ts * 128
                    residual_epilogue(R, ts, rows(hscr, r0), hres[r0 // 128], gpostA, rows(hscr, r0), hres[r0 // 128], next_sidx2())
            load_gpost(gpostA, 2, True)
            ffn_pass(R, ws, NT, hscr, lambda r0: hres[r0 // 128], y_d, lambda r0: yres, gpostA)

        P.emit_all(nc)
    return nc


_CACHE = {}


def _core_inputs(inputs, c, oh_da, oh_sw):
    xp = np.asarray(inputs["x_prompt"])
    xs = np.asarray(inputs["x_sample"])
    x = np.concatenate([xp[2 * c].reshape(-1, D), xp[2 * c + 1].reshape(-1, D), xs[c].reshape(-1, D)], axis=0)
    m = {"x": np.ascontiguousarray(x, dtype=np.float32)}
    for k in ("w_ffn1_gu", "w_ffn2_gu", "w_ffn1_down", "w_ffn2_down", "w_in", "w_out"):
        m[k] = np.ascontiguousarray(np.asarray(inputs[k])[0], dtype=np.float32)
    m["rel_bias"] = np.ascontiguousarray(np.asarray(inputs["rel_bias"]), dtype=np.float32)
    for k in ("g_ffn1_pre", "g_mix_pre", "g_ffn2_pre", "g_ffn1_post", "g_mix_post", "g_ffn2_post",
              "lambda_q1", "lambda_k1", "lambda_q2", "lambda_k2", "g_diff_subln", "sink_logit"):
        m[k] = np.ascontiguousarray(np.asarray(inputs[k]), dtype=np.float32).reshape(1, -1)
    m["oh_da"] = oh_da
    m["oh_sw"] = oh_sw
    return m


def kernel(**inputs):
    seqs = [2048, 2048, 4096]
    if "nc" not in _CACHE:
        _CACHE["nc"] = build(seqs)
    nc = _CACHE["nc"]
    oh_da, oh_sw = _onehot_tables()
    in_maps = [_core_inputs(inputs, c, oh_da, oh_sw) for c in range(8)]
    res = run_bass_kernel_spmd(nc, in_maps, core_ids=list(range(8)))
    yp = np.empty((16, 2048, D), np.float32)
    ys = np.empty((8, 4096, D), np.float32)
    for c in range(8):
        y = np.asarray(res.results[c]["y"])
        yp[2 * c] = y[0:2048]
        yp[2 * c + 1] = y[2048:4096]
        ys[c] = y[4096:8192]
    return (yp, ys)
```
